# Optimizing a Trainium2 kernel written in Bass

```python
import jax, jax.numpy as jnp
from jax import lax
import numpy as np

D_MODEL = 1024
BATCH = 8
SEQ = 4096
DEPTH = 2

CHUNK = 128
N_SGU_GROUPS = 8
SGU_WIDTH = D_MODEL
SGU_GROUP_DIM = SGU_WIDTH // N_SGU_GROUPS
N_HEADS = 8
HEAD_DIM = 128
ATTN_WIDTH = N_HEADS * HEAD_DIM
Q_BLOCK = 128
D_FF = -(-8 * D_MODEL // (3 * 256)) * 256
EPS = 1e-6

_IN_SIZES = (SGU_WIDTH, SGU_WIDTH, ATTN_WIDTH, ATTN_WIDTH, ATTN_WIDTH, D_MODEL, D_MODEL, N_HEADS)
IN_WIDTH = sum(_IN_SIZES)
IN_SPLITS = tuple(int(s) for s in np.cumsum(_IN_SIZES)[:-1])

kernel_name = "hybrid_gmlp_fox_gated_block"


def rmsnorm(x, g):
    xf = x.astype(jnp.float32)
    r = lax.rsqrt(jnp.mean(xf * xf, axis=-1, keepdims=True) + EPS)
    return (xf * r * g.astype(jnp.float32)).astype(x.dtype)


def layernorm(x, g):
    xf = x.astype(jnp.float32)
    mu = jnp.mean(xf, axis=-1, keepdims=True)
    xc = xf - mu
    r = lax.rsqrt(jnp.mean(xc * xc, axis=-1, keepdims=True) + EPS)
    return (xc * r * g.astype(jnp.float32)).astype(x.dtype)


def spatial_gating(u, v, w_s, b_s, g_v):
    bsz, seq = v.shape[0], v.shape[1]
    v = layernorm(v, g_v)
    vc = v.reshape(bsz, seq // CHUNK, CHUNK, N_SGU_GROUPS, SGU_GROUP_DIM)
    causal = jnp.tril(jnp.ones((CHUNK, CHUNK), dtype=bool))
    w = jnp.where(causal[None], w_s, jnp.zeros_like(w_s))
    mixed = jnp.einsum('gts,bcsgd->bctgd', w, vc) + b_s.T[None, None, :, :, None]
    return u * mixed.reshape(bsz, seq, SGU_WIDTH)


def forgetting_attention(q, k, v, f_logit, b_f):
    bsz, seq = q.shape[0], q.shape[1]

    def heads(t):
        return t.reshape(bsz, seq, N_HEADS, HEAD_DIM).transpose(0, 2, 1, 3)

    q = heads(q) * (HEAD_DIM ** -0.5)
    k = heads(k)
    v = heads(v)
    log_f = jax.nn.log_sigmoid((f_logit + b_f).astype(jnp.float32))
    c = jnp.cumsum(log_f, axis=1).transpose(0, 2, 1)
    k_pos = jnp.arange(seq)

    def block(i):
        start = i * Q_BLOCK
        qi = lax.dynamic_slice_in_dim(q, start, Q_BLOCK, axis=2)
        ci = lax.dynamic_slice_in_dim(c, start, Q_BLOCK, axis=2)
        s = jnp.einsum('bhqd,bhkd->bhqk', qi, k).astype(jnp.float32)
        s = s + ci[..., None] - c[:, :, None, :]
        q_pos = start + jnp.arange(Q_BLOCK)
        s = jnp.where(k_pos[None, :] <= q_pos[:, None], s, -jnp.inf)
        p = jax.nn.softmax(s, axis=-1).astype(v.dtype)
        return jnp.einsum('bhqk,bhkd->bhqd', p, v)

    out = lax.map(block, jnp.arange(seq // Q_BLOCK))
    return out.transpose(1, 0, 3, 2, 4).reshape(bsz, seq, ATTN_WIDTH)


def swiglu(h, w_gate, w_up, w_down):
    return (jax.nn.silu(h @ w_gate) * (h @ w_up)) @ w_down


def setup_inputs(seed: int = 0) -> dict:
    key = jax.random.key(seed)
    ks = jax.random.split(key, 15)
    f32 = jnp.float32

    def nrm(k, shape, scale):
        return jax.random.normal(k, shape, f32) * scale

    def gain(k, shape):
        return 1.0 + 0.1 * jax.random.normal(k, shape, f32)

    x = jax.random.normal(ks[0], (BATCH, SEQ, D_MODEL), f32)
    mix_pre_g = gain(ks[1], (DEPTH, D_MODEL))
    w_in = nrm(ks[2], (DEPTH, D_MODEL, IN_WIDTH), D_MODEL ** -0.5)
    b_forget = jnp.linspace(1.0, 6.0, N_HEADS, dtype=f32)[None, :] + 0.1 * jax.random.normal(ks[3], (DEPTH, N_HEADS), f32)
    sgu_norm_g = gain(ks[4], (DEPTH, SGU_WIDTH))
    w_spatial = nrm(ks[5], (DEPTH, N_SGU_GROUPS, CHUNK, CHUNK), CHUNK ** -0.5)
    b_spatial = 1.0 + 0.1 * jax.random.normal(ks[6], (DEPTH, N_SGU_GROUPS, CHUNK), f32)
    w_out = nrm(ks[7], (DEPTH, D_MODEL, D_MODEL), D_MODEL ** -0.5)
    mix_post_g = gain(ks[8], (DEPTH, D_MODEL))
    ffn_pre_g = gain(ks[9], (DEPTH, D_MODEL))
    w_gate = nrm(ks[10], (DEPTH, D_MODEL, D_FF), D_MODEL ** -0.5)
    w_up = nrm(ks[11], (DEPTH, D_MODEL, D_FF), D_MODEL ** -0.5)
    w_down = nrm(ks[12], (DEPTH, D_FF, D_MODEL), D_FF ** -0.5)
    ffn_post_g = gain(ks[13], (DEPTH, D_MODEL))
    return {"x": x, "mix_pre_g": mix_pre_g, "w_in": w_in, "b_forget": b_forget,
            "sgu_norm_g": sgu_norm_g, "w_spatial": w_spatial, "b_spatial": b_spatial,
            "w_out": w_out, "mix_post_g": mix_post_g, "ffn_pre_g": ffn_pre_g,
            "w_gate": w_gate, "w_up": w_up, "w_down": w_down, "ffn_post_g": ffn_post_g}


def reference(x, mix_pre_g, w_in, b_forget, sgu_norm_g, w_spatial, b_spatial, w_out,
              mix_post_g, ffn_pre_g, w_gate, w_up, w_down, ffn_post_g):
    for l in range(DEPTH):
        h = rmsnorm(x, mix_pre_g[l])
        proj = h @ w_in[l]
        u, v_s, q, k, v_a, g_a, g_b, f_logit = jnp.split(proj, IN_SPLITS, axis=-1)
        y_a = spatial_gating(jax.nn.gelu(u), jax.nn.gelu(v_s), w_spatial[l], b_spatial[l], sgu_norm_g[l])
        y_b = forgetting_attention(q, k, v_a, f_logit, b_forget[l])
        merged = jax.nn.sigmoid(g_a) * y_a + jax.nn.sigmoid(g_b) * y_b
        x = x + rmsnorm(merged @ w_out[l], mix_post_g[l])
        h = rmsnorm(x, ffn_pre_g[l])
        x = x + rmsnorm(swiglu(h, w_gate[l], w_up[l], w_down[l]), ffn_post_g[l])
    return x
```

```python
import contextlib
import numpy as np
import concourse.bass as bass
import concourse.mybir as mybir
from concourse.bass_utils import run_bass_kernel_spmd

F32 = mybir.dt.float32
BF16 = mybir.dt.bfloat16
AF = mybir.ActivationFunctionType
ALU = mybir.AluOpType

D = 1024
NH = 8
DFF = 2816
NFC = DFF // 128
INW = 7176
EPS = 1e-6


class Prog:
    def __init__(self, nc):
        self.nc = nc
        self.engs = ['pe', 'act', 'dve', 'pool', 'sp']
        self.stream = {k: [] for k in self.engs}
        self.sem = {}
        self.cnt = {}
        self.lastw = {}
        self.readers = {}
        self.known = {e: {} for e in self.engs}
        self.clock_at = {}
        for k in self.engs:
            self._mksem(k)

    def _mksem(self, key):
        self.sem[key] = self.nc.alloc_semaphore("s_" + key)
        self.cnt[key] = 0

    def _need(self, e, key, val):
        if e == 'pe' and key == 'pe':
            return
        kn = self.known[e]
        if kn.get(key, 0) >= val:
            return
        sem = self.sem[key]
        self.stream[e].append(lambda E, sem=sem, val=val: E.wait_ge(sem, val))
        kn[key] = val
        snap = self.clock_at.get((key, val))
        if snap:
            for k2, v2 in snap.items():
                if kn.get(k2, 0) < v2:
                    kn[k2] = v2

    def op(self, e, fn, reads=(), writes=(), inc=True, dma=None):
        for r in reads:
            w = self.lastw.get(r)
            if w:
                self._need(e, *w)
        for w_ in writes:
            w = self.lastw.get(w_)
            if w:
                self._need(e, *w)
            for k, v in self.readers.get(w_, {}).items():
                self._need(e, k, v)
        if dma is not None:
            key = dma
            if key not in self.sem:
                self._mksem(key)
            self.cnt[key] += 16
            val = self.cnt[key]
            sem = self.sem[key]
            self.stream[e].append(lambda E, fn=fn, sem=sem: fn(E).then_inc(sem, 16))
            self.clock_at[(key, val)] = dict(self.known[e])
        else:
            key = e
            if inc:
                self.cnt[key] += 1
                val = self.cnt[key]
                sem = self.sem[key]
                self.stream[e].append(lambda E, fn=fn, sem=sem: fn(E).then_inc(sem, 1))
                self.clock_at[(key, val)] = dict(self.known[e])
            else:
                assert e == 'pe'
                val = self.cnt[key] + 1
                self.stream[e].append(lambda E, fn=fn: fn(E))
        for r in reads:
            d = self.readers.setdefault(r, {})
            d[key] = max(d.get(key, 0), val)
        for w_ in writes:
            self.lastw[w_] = (key, val)
            self.readers[w_] = {}

    def barrier(self, engines=None):
        for e in (engines or self.engs):
            for k in list(self.sem.keys()):
                if self.cnt[k] > 0:
                    self._need(e, k, self.cnt[k])

    def emit(self):
        nc = self.nc
        with nc.Block() as block:
            @block.tensor
            def _(E):
                for f in self.stream['pe']:
                    f(E)

            @block.scalar
            def _(E):
                for f in self.stream['act']:
                    f(E)

            @block.vector
            def _(E):
                for f in self.stream['dve']:
                    f(E)

            @block.gpsimd
            def _(E):
                for f in self.stream['pool']:
                    f(E)

            @block.sync
            def _(E):
                for f in self.stream['sp']:
                    f(E)


def build_program(S=4096, depth=2, debug=False):
    NT = S // 128
    NG = S // 512
    nc = bass.Bass("TRN2", target_bir_lowering=False)
    P = Prog(nc)

    def din(name, shape, dt=F32):
        return nc.dram_tensor(name, shape, dt, kind="ExternalInput").ap()

    x_in = din("x", [S, D])
    mix_pre_g = din("mix_pre_g", [depth, D])
    w_in = din("w_in", [depth, D, INW])
    b_forget = din("b_forget", [depth, NH])
    sgu_norm_g = din("sgu_norm_g", [depth, D])
    w_spatial = din("w_spatial", [depth, 8, 128, 128])
    b_spatial = din("b_spatial", [depth, 8, 128])
    w_out = din("w_out", [depth, D, D])
    mix_post_g = din("mix_post_g", [depth, D])
    ffn_pre_g = din("ffn_pre_g", [depth, D])
    w_gate = din("w_gate", [depth, D, DFF])
    w_up = din("w_up", [depth, D, DFF])
    w_down = din("w_down", [depth, DFF, D])
    ffn_post_g = din("ffn_post_g", [depth, D])
    c_ident = din("c_ident", [128, 128])
    c_tril = din("c_tril", [128, 128])
    c_triu = din("c_triu", [128, 128])
    y_out = nc.dram_tensor("y", [S, D], F32, kind="ExternalOutput").ap()

    skind = "ExternalOutput" if debug else "Internal"
    q_scr = nc.dram_tensor("q_scr", [NH, 128, S], BF16, kind=skind).ap()
    k_scr = nc.dram_tensor("k_scr", [NH, 128, S], BF16, kind=skind).ap()
    v_scr = nc.dram_tensor("v_scr", [NH, S, 128], BF16, kind=skind).ap()
    c_scr = nc.dram_tensor("c_scr", [NH, S], F32, kind=skind).ap()
    yb_scr = nc.dram_tensor("yb_scr", [S, D], BF16, kind=skind).ap()
    x1_scr = nc.dram_tensor("x1_scr", [S, D], F32, kind=skind).ap()
    x2_scr = nc.dram_tensor("x2_scr", [S, D], F32, kind=skind).ap()

    def sb(name, shape, dt):
        return nc.alloc_sbuf_tensor(name, shape, dt)

    idf = sb("idf", [128, 128], F32)
    idb = sb("idb", [128, 128], BF16)
    trilf = sb("trilf", [128, 128], F32)
    triuf = sb("triuf", [128, 128], F32)
    negmask = sb("negmask", [128, 128], F32)
    mhalf = sb("mhalf", [128, 8], F32)
    zeros = sb("zeros", [128, 128], F32)
    bsT = sb("bsT", [128, 8], F32)
    negbf = sb("negbf", [8, 1], F32)
    negcT = sb("negcT", [128, NT, 8], F32)
    junk = sb("junk", [128, D], BF16)
    ones8 = sb("ones8", [8, 512], F32)

    def dma(e, out, in_, reads, writes, key, slow=False):
        if slow:
            P.op(e, lambda E: E.dma_start(out=out, in_=in_, allow_slow_non_contiguous=True), reads, writes, dma=key)
        else:
            P.op(e, lambda E: E.dma_start(out=out, in_=in_), reads, writes, dma=key)

    def mm(out, lhsT, rhs, start, stop, reads, writes, inc, skip=False):
        if skip:
            P.op('pe', lambda E: E.matmul(out, lhsT=lhsT, rhs=rhs, start=start, stop=stop, skip_group_check=True),
                 reads, writes, inc=inc)
        else:
            P.op('pe', lambda E: E.matmul(out, lhsT=lhsT, rhs=rhs, start=start, stop=stop), reads, writes, inc=inc)

    def tr(out, in_, ident, reads, writes, inc):
        P.op('pe', lambda E: E.transpose(out=out, in_=in_, identity=ident), reads, writes, inc=inc)

    def act(out, in_, func, reads, writes, bias=None, scale=1.0, accum=None):
        kw = {}
        if bias is not None:
            kw['bias'] = bias
        if accum is not None:
            kw['accum_out'] = accum
        P.op('act', lambda E: E.activation(out=out, in_=in_, func=func, scale=scale, **kw), reads, writes)

    def tt(e, out, in0, in1, op, reads, writes):
        P.op(e, lambda E: E.tensor_tensor(out=out, in0=in0, in1=in1, op=op), reads, writes)

    def ts(e, out, in0, s1, s2, op0, op1, reads, writes):
        if s2 is None:
            P.op(e, lambda E: E.tensor_scalar(out=out, in0=in0, scalar1=s1, scalar2=None, op0=op0), reads, writes)
        else:
            P.op(e, lambda E: E.tensor_scalar(out=out, in0=in0, scalar1=s1, scalar2=s2, op0=op0, op1=op1), reads, writes)

    def stt(e, out, in0, scalar, in1, op0, op1, reads, writes):
        P.op(e, lambda E: E.scalar_tensor_tensor(out=out, in0=in0, scalar=scalar, in1=in1, op0=op0, op1=op1),
             reads, writes)

    def cp(e, out, in_, reads, writes):
        if e == 'act':
            act(out, in_, AF.Copy, reads, writes)
        else:
            P.op(e, lambda E: E.tensor_copy(out=out, in_=in_), reads, writes)

    def memset(e, ap, val, writes):
        P.op(e, lambda E: E.memset(ap, val), (), writes)

    def rstd_from_ss(ss_ap, ms_ap, out_ap, n, eps, res):
        ts('dve', ms_ap, ss_ap, 1.0 / D, eps, ALU.mult, ALU.add, [res], [res])
        tt('pool', out_ap, ms_ap, mhalf[:, 0:n], ALU.pow, [res, 'mhalf'], [res])

    dma('sp', idf[:], c_ident, [], ['idf'], 'd_c0')
    dma('sp', trilf[:], c_tril, [], ['trilf'], 'd_c1')
    dma('sp', triuf[:], c_triu, [], ['triuf'], 'd_c2')
    cp('dve', idb[:], idf[:], ['idf'], ['idb'])
    ts('dve', negmask[:], triuf[:], -1.0, 30000.0, ALU.add, ALU.mult, ['triuf'], ['negmask'])
    memset('pool', mhalf[:], -0.5, ['mhalf'])
    memset('pool', zeros[:], 0.0, ['zeros'])
    memset('pool', ones8[:], 1.0, ['ones8'])

    def load_w(dst_ap, src_ap, res, key, slow=False):
        dma('pool', dst_ap, src_ap, [], [res], key, slow=slow)

    for l in range(depth):
        x_src = x_in if l == 0 else x2_scr
        x_dst = y_out if l == depth - 1 else x2_scr

        with contextlib.ExitStack() as es:
            def sbt(name, shape, dt):
                return es.enter_context(nc.sbuf_tensor(f"{name}_{l}", shape, dt))

            def pst(name, shape, dt=F32):
                return es.enter_context(nc.psum_tensor(f"{name}_{l}", shape, dt))

            W1 = sbt("p1W1", [128, 8, 3080], BF16)
            gb_pre = sbt("p1gbpre", [128, D], F32)
            xin = [sbt(f"p1xin{i}", [128, 4, D], F32) for i in range(2)]
            xn = sbt("p1xn", [128, 4, D], BF16)
            hT = [sbt(f"p1hT{i}", [128, 8, 512], BF16) for i in range(2)]
            ss = sbt("p1ss", [128, 16], F32)
            qk_sb = [sbt(f"p1qk{i}", [128, 512], BF16) for i in range(4)]
            v_sb = [sbt(f"p1v{i}", [128, D], BF16) for i in range(2)]
            ef_t = sbt("p1ef", [128, 512], F32)
            lf_t = sbt("p1lf", [128, 512], F32)
            cg_t = [sbt(f"p1cg{i}", [128, 512], F32) for i in range(2)]
            dma('sp', ef_t[:], x_in[0:128, 0:512], [], ['ef'], 'd_i0')
            dma('sp', lf_t[:], x_in[0:128, 0:512], [], ['lf'], 'd_i1')
            dma('sp', cg_t[0][:], x_in[0:128, 0:512], [], ['cg0'], 'd_i2')
            dma('sp', cg_t[1][:], x_in[0:128, 0:512], [], ['cg1'], 'd_i3')
            ef = ef_t[0:8, :]
            lf = lf_t[0:8, :]
            cg = [cg_t[0][0:8, :], cg_t[1][0:8, :]]
            psT = [pst(f"p1psT{i}", [128, 8, 128], BF16) for i in range(2)]
            psA = [pst(f"p1psA{i}", [128, 512]) for i in range(4)]
            psF = pst("p1psF", [8, 512])
            psC = pst("p1psC", [128, 4, 8])

            load_w(W1[:, :, 3072:3080], w_in[l, :, 7168:INW].rearrange("(k p) n -> p k n", p=128), 'W1c6', 'd_w6',
                   slow=True)
            for ci in range(6):
                load_w(W1[:, :, ci * 512:(ci + 1) * 512],
                       w_in[l, :, 2048 + ci * 512:2048 + (ci + 1) * 512].rearrange("(k p) n -> p k n", p=128),
                       f'W1c{ci}', f'd_w{ci}')
            dma('sp', gb_pre[:], mix_pre_g[l].partition_broadcast(128), [], ['gb_pre'], 'd_g0')
            dma('sp', negbf[:], b_forget[l].rearrange("(h o) -> h o", o=1), [], ['negbf'], 'd_g6', slow=True)
            ts('dve', negbf[:], negbf[:], -1.0, None, ALU.mult, None, ['negbf'], ['negbf'])

            na = [0]

            def p1_load(gi):
                s2 = gi % 2
                dma('sp', xin[s2][:], x_src[gi * 512:(gi + 1) * 512, :].rearrange("(t p) d -> p t d", p=128),
                    [], [f'xin{s2}'], f'd_xin{s2}')

            def p1_stats(gi):
                s2 = gi % 2
                XIN = f'xin{s2}'
                for t in range(4):
                    act(junk[:], xin[s2][:, t, :], AF.Square, [XIN], ['junk', 'ss'], accum=ss[:, t:t + 1])
                rstd_from_ss(ss[:, 0:4], ss[:, 4:8], ss[:, 8:12], 4, EPS, 'ss')
                for t in range(4):
                    stt('dve', xn[:, t, :], xin[s2][:, t, :], ss[:, 8 + t:9 + t], gb_pre[:], ALU.mult, ALU.mult,
                        [XIN, 'ss', 'gb_pre'], [f'xn{t}'])

            def p1_tr(gi):
                s2 = gi % 2
                HT = f'hT{s2}'
                for t in range(4):
                    pt = t % 2
                    for k in range(8):
                        tr(psT[pt][:, k, :], xn[:, t, k * 128:(k + 1) * 128], idb[:], [f'xn{t}', 'idb'], [f'psT{pt}'],
                           inc=(k == 7))
                    cp('act' if t % 2 else 'dve', hT[s2][:, :, t * 128:(t + 1) * 128], psT[pt][:],
                       [f'psT{pt}'], [HT])

            def p1_qk(gi, n0, n1):
                s2 = gi % 2
                HT = f'hT{s2}'
                for n in range(n0, n1):
                    b = na[0] % 4
                    na[0] += 1
                    for k in range(8):
                        mm(psA[b][:], W1[:, k, n * 128:(n + 1) * 128], hT[s2][:, k, :], k == 0, k == 7,
                           [f'W1c{n // 4}', HT], [f'psA{b}'], inc=(k == 7))
                    if n < 8:
                        act(qk_sb[b][:], psA[b][:], AF.Copy, [f'psA{b}'], [f'qk{b}'], scale=float(128 ** -0.5))
                        dma('sp', q_scr[n, :, gi * 512:(gi + 1) * 512], qk_sb[b][:], [f'qk{b}'], [], f'd_qk{b}')
                    else:
                        cp('dve', qk_sb[b][:], psA[b][:], [f'psA{b}'], [f'qk{b}'])
                        dma('sp', k_scr[n - 8, :, gi * 512:(gi + 1) * 512], qk_sb[b][:], [f'qk{b}'], [],
                            f'd_qk{b}')

            def p1_v(gi):
                s2 = gi % 2
                HT = f'hT{s2}'
                for t in range(4):
                    vs = t % 2
                    for n2 in range(2):
                        b = na[0] % 4
                        na[0] += 1
                        for k in range(8):
                            mm(psA[b][:], hT[s2][:, k, t * 128:(t + 1) * 128],
                               W1[:, k, 2048 + n2 * 512:2048 + (n2 + 1) * 512], k == 0, k == 7,
                               [f'W1c{4 + n2}', HT], [f'psA{b}'], inc=(k == 7))
                        cp('act' if n2 else 'dve', v_sb[vs][:, n2 * 512:(n2 + 1) * 512], psA[b][:],
                           [f'psA{b}'], [f'v{vs}'])
                    tok0 = gi * 512 + t * 128
                    dma('sp', v_scr[:, tok0:tok0 + 128, :].rearrange("h t d -> t h d"),
                        v_sb[vs][:].rearrange("p (h d) -> p h d", h=8), [f'v{vs}'], [], f'd_v{vs}')

            def p1_f(gi):
                s2 = gi % 2
                HT = f'hT{s2}'
                for k in range(8):
                    mm(psF[:], W1[:, k, 3072:3080], hT[s2][:, k, :], k == 0, k == 7, ['W1c6', HT], ['psF'],
                       inc=(k == 7))
                act(ef, psF[:], AF.Exp, ['psF', 'negbf'], ['ef'], bias=negbf[:, 0:1], scale=-1.0)
                act(lf, ef, AF.Ln, ['ef'], ['lf'], bias=1.0)
                init = 0.0 if gi == 0 else cg[1 - s2][:, 511:512]
                P.op('dve', lambda E, o=cg[s2], i=init, lf_=lf, on_=ones8[:]: E.tensor_tensor_scan(
                    out=o, data0=on_, data1=lf_, initial=i, op0=ALU.mult, op1=ALU.subtract),
                    ['ones8', 'lf', f'cg{1 - s2}'], [f'cg{s2}'])
                dma('sp', c_scr[:, gi * 512:(gi + 1) * 512], cg[s2], [f'cg{s2}'], [], f'd_cg{s2}')

            def p1_cgT(gi):
                s2 = gi % 2
                for j in range(4):
                    tr(psC[:, j, :], cg[s2][:, j * 128:(j + 1) * 128], idf[0:8, 0:8], [f'cg{s2}', 'idf'], ['psC'],
                       inc=(j == 3))
                ts('dve', negcT[:, gi * 4:(gi + 1) * 4, :], psC[:], -1.0, None, ALU.mult, None, ['psC'], ['negcT'])

            p1_load(0)
            p1_stats(0)
            p1_tr(0)
            for gi in range(NG):
                if gi + 1 < NG:
                    p1_load(gi + 1)
                p1_f(gi)
                p1_qk(gi, 0, 6)
                if gi + 1 < NG:
                    p1_stats(gi + 1)
                p1_qk(gi, 6, 16)
                if gi > 0:
                    p1_cgT(gi - 1)
                if gi + 1 < NG:
                    p1_tr(gi + 1)
                p1_v(gi)
            p1_cgT(NG - 1)
            P.barrier()

        es23 = contextlib.ExitStack()
        es23.__enter__()

        def sbt23(name, shape, dt):
            return es23.enter_context(nc.sbuf_tensor(f"{name}_{l}", shape, dt))

        W3 = sbt23("p3W3", [128, 8, 4096], BF16)
        Wo = sbt23("p3Wo", [128, 8, D], BF16)
        gb_pre3 = sbt23("p3gbpre", [128, D], F32)
        gb_post = sbt23("p3gbpost", [128, D], F32)
        gvb = sbt23("p3gvb", [128, D], F32)

        def load_p3_weights():
            for n in range(8):
                c_src = n * 512 if n < 4 else 5120 + (n - 4) * 512
                load_w(W3[:, :, n * 512:(n + 1) * 512],
                       w_in[l, :, c_src:c_src + 512].rearrange("(k p) n -> p k n", p=128), f'W3c{n}', f'd_w{n}')
            load_w(Wo[:], w_out[l].rearrange("(k p) n -> p k n", p=128), 'Wo', 'd_w8')
            dma('sp', gb_pre3[:], mix_pre_g[l].partition_broadcast(128), [], ['gb_pre3'], 'd_g0')
            dma('sp', gb_post[:], mix_post_g[l].partition_broadcast(128), [], ['gb_post'], 'd_g1')
            dma('sp', gvb[:], sgu_norm_g[l].partition_broadcast(128), [], ['gvb'], 'd_g4')

        with contextlib.ExitStack() as es:
            def sbt(name, shape, dt):
                return es.enter_context(nc.sbuf_tensor(f"{name}_{l}", shape, dt))

            def pst(name, shape, dt=F32):
                return es.enter_context(nc.psum_tensor(f"{name}_{l}", shape, dt))

            qT = [sbt(f"p2q{i}", [128, S], BF16) for i in range(2)]
            kT = [sbt(f"p2k{i}", [128, S], BF16) for i in range(2)]
            V = [sbt(f"p2v{i}", [128, NT, 129], BF16) for i in range(2)]
            cbc = [sbt(f"p2c{i}", [128, 1024], F32) for i in range(3)]
            cbcD = [sbt(f"p2cD{i}", [128, 1024], F32) for i in range(3)]
            sadd = [sbt(f"p2sa{i}", [128, 1024], F32) for i in range(4)]
            pT = [sbt(f"p2pT{i}", [128, 1024], BF16) for i in range(3)]
            yb_sb = [sbt(f"p2yb{i}", [128, 4, 128], BF16) for i in range(4)]
            rinv = sbt("p2rinv", [128, 8], F32)
            psS = [pst(f"p2psS{i}", [128, 1024]) for i in range(2)]
            po = [pst(f"p2po{i}", [128, 2, 256]) for i in range(4)]

            for s2 in range(2):
                memset('pool', V[s2][:, :, 128:129], 1.0, [f'V{s2}'])

            def load_head(h):
                s2 = h % 2
                dma('sp', qT[s2][:], q_scr[h], [], [f'qT{s2}'], f'd_q{s2}')
                dma('sp', kT[s2][:], k_scr[h], [], [f'kT{s2}'], f'd_k{s2}')
                dma('sp', V[s2][:, :, 0:128], v_scr[h].rearrange("(kb p) d -> p kb d", p=128), [],
                    [f'V{s2}'], f'd_vv{s2}')

            def load_cbc(rd):
                h, jp = rd // (NG // 2), rd % (NG // 2)
                c3 = rd % 3
                dma('sp', cbc[c3][:], c_scr[h, jp * 1024:(jp + 1) * 1024].partition_broadcast(128), [],
                    [f'cbc{c3}'], f'd_c{c3}')
                for tix in range(8):
                    tt('pool', cbcD[c3][:, tix * 128:(tix + 1) * 128], cbc[c3][:, tix * 128:(tix + 1) * 128], negmask[:],
                       ALU.add, [f'cbc{c3}', 'negmask'], [f'cbcD{c3}'])

            steps = []
            for h in range(NH):
                for jp in range(NG // 2):
                    j0, j1 = 2 * jp, 2 * jp + 1
                    for kb in range(4 * j1 + 4):
                        steps.append((h, j0, j1, kb, kb < 4 * j0 + 4))
            NS = len(steps)

            def geom(n):
                h, j0, j1, kb, has0 = steps[n]
                r0 = kb - 4 * j0
                r1 = kb - 4 * j1
                c0 = 128 * max(r0, 0)
                c1 = 128 * max(r1, 0)
                lo = c0 if has0 else 512 + c1
                return h, j0, j1, kb, has0, r0, r1, c0, c1, lo

            def emit_qk(n):
                h, j0, j1, kb, has0, r0, r1, c0, c1, lo = geom(n)
                s2 = h % 2
                b = n % 2
                ks = kT[s2][:, kb * 128:(kb + 1) * 128]
                if has0:
                    mm(psS[b][:, c0:512], ks, qT[s2][:, j0 * 512 + c0:(j0 + 1) * 512],
                       True, True, [f'kT{s2}', f'qT{s2}'], [f'psS{b}'], inc=False)
                mm(psS[b][:, 512 + c1:1024], ks, qT[s2][:, j1 * 512 + c1:(j1 + 1) * 512],
                   True, True, [f'kT{s2}', f'qT{s2}'], [f'psS{b}'], inc=True)

            def emit_add(n):
                h, j0, j1, kb, has0, r0, r1, c0, c1, lo = geom(n)
                b = n % 2
                a4 = n % 4
                c3 = (h * (NG // 2) + j0 // 2) % 3
                d0 = None
                if has0 and r0 >= 0:
                    d0 = c0
                elif r1 >= 0:
                    d0 = 512 + c1
                if d0 is not None:
                    assert d0 == lo
                    tt('dve', sadd[a4][:, d0:d0 + 128], psS[b][:, d0:d0 + 128], cbcD[c3][:, d0:d0 + 128], ALU.add,
                       [f'psS{b}', f'cbcD{c3}'], [f'sadd{a4}'])
                    lo2 = d0 + 128
                else:
                    lo2 = lo
                if lo2 < 1024:
                    tt('dve', sadd[a4][:, lo2:1024], psS[b][:, lo2:1024], cbc[c3][:, lo2:1024], ALU.add,
                       [f'psS{b}', f'cbc{c3}'], [f'sadd{a4}'])

            def emit_tail(n):
                h, j0, j1, kb, has0, r0, r1, c0, c1, lo = geom(n)
                s2 = h % 2
                a4 = n % 4
                a3 = n % 3
                act(pT[a3][:, lo:1024], sadd[a4][:, lo:1024], AF.Exp, [f'sadd{a4}', 'negcT'], [f'pT{a3}'],
                    bias=negcT[:, kb, h:h + 1])
                tiles = []
                if has0:
                    tiles += [(0, i) for i in range(max(r0, 0), 4)]
                tiles += [(1, i) for i in range(max(r1, 0), 4)]
                for idx, (jj, i) in enumerate(tiles):
                    u = 4 * jj + i
                    bank, slot = u // 2, u % 2
                    first = (kb == 0 and slot == 0)
                    mm(po[bank][:, slot, 0:129], pT[a3][:, jj * 512 + i * 128:jj * 512 + (i + 1) * 128], V[s2][:, kb, :],
                       first, False, [f'pT{a3}', f'V{s2}'], [f'po{bank}'], inc=(idx == len(tiles) - 1), skip=True)

            def emit_evac(n):
                h, j0, j1, kb, has0, r0, r1, c0, c1, lo = geom(n)
                done = []
                if has0 and r0 in (1, 3):
                    done.append((0, r0, j0))
                if r1 in (1, 3):
                    done.append((1, r1, j1))
                for jj, r, j in done:
                    ys = (h * NG + j) % 4
                    for i in (r - 1, r):
                        u = 4 * jj + i
                        bank, slot = u // 2, u % 2
                        P.op('dve', lambda E, o_=rinv[:, u:u + 1], i_=po[bank][:, slot, 128:129]: E.reciprocal(
                            out=o_, in_=i_), [f'po{bank}'], [f'rinv{u}'])
                        act(yb_sb[ys][:, i, :], po[bank][:, slot, 0:128], AF.Copy, [f'po{bank}', f'rinv{u}'],
                            [f'yb{ys}'], scale=rinv[:, u:u + 1])
                    if r == 3:
                        dma('sp', yb_scr[j * 512:(j + 1) * 512, h * 128:(h + 1) * 128].rearrange("(i p) d -> p i d", p=128),
                            yb_sb[ys][:], [f'yb{ys}'], [], f'd_yb{ys}')

            def round_of(n):
                h, j0 = steps[n][0], steps[n][1]
                return h * (NG // 2) + j0 // 2

            NR = NH * (NG // 2)
            AHEAD = 2
            load_head(0)
            load_cbc(0)
            if NR > 1:
                load_cbc(1)
            load_p3_weights()
            cur_head, cur_round = 0, 1
            for n in range(min(AHEAD, NS)):
                emit_qk(n)
                emit_add(n)
            for n in range(NS):
                h = steps[n][0]
                if h + 1 < NH and cur_head < h + 1:
                    load_head(h + 1)
                    cur_head = h + 1
                rd = round_of(n)
                if rd + 2 < NR and cur_round < rd + 2:
                    load_cbc(rd + 2)
                    cur_round = rd + 2
                if n + AHEAD < NS:
                    emit_qk(n + AHEAD)
                    emit_add(n + AHEAD)
                new_round = (steps[n][3] == 0)
                if n >= 1 and new_round:
                    emit_evac(n - 1)
                emit_tail(n)
                if n >= 1 and not new_round:
                    emit_evac(n - 1)
            emit_evac(NS - 1)
            P.barrier()

        with contextlib.ExitStack() as es:
            def sbt(name, shape, dt):
                return es.enter_context(nc.sbuf_tensor(f"{name}_{l}", shape, dt))

            def pst(name, shape, dt=F32):
                return es.enter_context(nc.psum_tensor(f"{name}_{l}", shape, dt))

            wsT = sbt("p3wsT", [128, 8, 128], BF16)
            xin = [sbt(f"p3xin{i}", [128, D], F32) for i in range(4)]
            ybt = [sbt(f"p3yb{i}", [128, D], BF16) for i in range(3)]
            xn = [sbt(f"p3xn{i}", [128, D], BF16) for i in range(2)]
            hT = [sbt(f"p3hT{i}", [128, 8, 128], BF16) for i in range(2)]
            ss = [sbt(f"p3ss{i}", [128, 16], F32) for i in range(4)]
            st = [sbt(f"p3st{i}", [128, 16], F32) for i in range(2)]
            ug = [sbt(f"p3ug{i}", [128, D], F32) for i in range(2)]
            vg = [sbt(f"p3vg{i}", [128, D], F32) for i in range(2)]
            vh = [sbt(f"p3vh{i}", [128, D], BF16) for i in range(1)]
            ta = [sbt(f"p3ta{i}", [128, D], F32) for i in range(2)]
            tb = [sbt(f"p3tb{i}", [128, D], F32) for i in range(2)]
            t1 = [sbt(f"p3t1{i}", [128, D], F32) for i in range(1)]
            t2 = [sbt(f"p3t2{i}", [128, D], F32) for i in range(2)]
            m2 = [sbt(f"p3m2{i}", [128, D], BF16) for i in range(2)]
            mT = [sbt(f"p3mT{i}", [128, 8, 128], BF16) for i in range(1)]
            xo = [sbt(f"p3xo{i}", [128, D], F32) for i in range(2)]
            wsf = t2[0][:].rearrange("p (g s) -> p g s", g=8)
            wsm = m2[0][:].rearrange("p (g s) -> p g s", g=8)
            psT = [pst(f"p3psT{i}", [128, 8, 128], BF16) for i in range(2)]
            psP = [pst(f"p3psP{i}", [128, 512]) for i in range(2)]
            psM = pst("p3psM", [128, D])
            psO = pst("p3psO", [128, D])

            dma('sp', bsT[:], b_spatial[l].rearrange("g t -> t g"), [], ['bsT'], 'd_g5', slow=True)
            dma('sp', wsf, w_spatial[l].rearrange("g t s -> t g s"), [], ['t20'], 'd_g7')
            for g in range(8):
                tt('dve', wsm[:, g, :], wsf[:, g, :], trilf[:], ALU.mult, ['t20', 'trilf'], ['m20'])
            for g in range(8):
                tr(psT[1][:, g, :], wsm[:, g, :], idb[:], ['m20', 'idb'], ['psT1'], inc=(g == 7))
            cp('dve', wsT[:], psT[1][:], ['psT1'], ['wsT'])

            pcount = [0]

            def proj(ti, n):
                s2 = ti % 2
                b = pcount[0] % 2
                pcount[0] += 1
                for k in range(8):
                    mm(psP[b][:], hT[s2][:, k, :], W3[:, k, n * 512:(n + 1) * 512], k == 0, k == 7,
                       [f'hT{s2}', f'W3c{n}'], [f'psP{b}'], inc=(k == 7))
                cs = slice((n % 2) * 512, (n % 2 + 1) * 512)
                if n < 2:
                    act(ug[s2][:, cs], psP[b][:], AF.Gelu_apprx_tanh, [f'psP{b}'], [f'ug{s2}'])
                elif n < 4:
                    act(vg[s2][:, cs], psP[b][:], AF.Gelu_apprx_tanh, [f'psP{b}'], [f'vg{s2}'])
                elif n < 6:
                    act(ta[s2][:, cs], psP[b][:], AF.Tanh, [f'psP{b}'], [f'ta{s2}'], scale=0.5)
                else:
                    act(tb[s2][:, cs], psP[b][:], AF.Tanh, [f'psP{b}'], [f'tb{s2}'], scale=0.5)

            def ok(t):
                return 0 <= t < NT

            def loads(t):
                if not ok(t):
                    return
                x4, y3 = t % 4, t % 3
                dma('sp', xin[x4][:], x_src[t * 128:(t + 1) * 128, :], [], [f'xin{x4}'], f'd_xin{x4}')
                dma('sp', ybt[y3][:], yb_scr[t * 128:(t + 1) * 128, :], [], [f'ybt{y3}'], f'd_ybt{y3}')

            def S0a(t):
                if not ok(t):
                    return
                x4, s2 = t % 4, t % 2
                XIN, SS = f'xin{x4}', f'ss{x4}'
                act(junk[:], xin[x4][:], AF.Square, [XIN], ['junk', SS], accum=ss[x4][:, 0:1])
                rstd_from_ss(ss[x4][:, 0:1], ss[x4][:, 1:2], ss[x4][:, 2:3], 1, EPS, SS)
                stt('dve', xn[s2][:], xin[x4][:], ss[x4][:, 2:3], gb_pre3[:], ALU.mult, ALU.mult,
                    [XIN, SS, 'gb_pre3'], [f'xn{s2}'])

            def S0b(t):
                if not ok(t):
                    return
                s2 = t % 2
                for k in range(8):
                    tr(psT[0][:, k, :], xn[s2][:, k * 128:(k + 1) * 128], idb[:], [f'xn{s2}', 'idb'], ['psT0'],
                       inc=(k == 7))
                cp('act', hT[s2][:], psT[0][:], ['psT0'], [f'hT{s2}'])

            def S1a(t):
                if ok(t):
                    for n in range(4):
                        proj(t, n)

            def S1b(t):
                if ok(t):
                    for n in range(4, 8):
                        proj(t, n)

            def S2a(t):
                if not ok(t):
                    return
                s2 = t % 2
                ST = f'st{s2}'
                P.op('dve', lambda E, o_=st[s2][:, 0:6], i_=vg[s2][:, 0:512]: E.bn_stats(out=o_, in_=i_),
                     [f'vg{s2}'], [ST])
                P.op('dve', lambda E, o_=st[s2][:, 6:12], i_=vg[s2][:, 512:1024]: E.bn_stats(out=o_, in_=i_),
                     [f'vg{s2}'], [ST])
                P.op('dve', lambda E, o_=st[s2][:, 12:14], i_=st[s2][:, 0:12]: E.bn_aggr(out=o_, in_=i_), [ST], [ST])
                ts('dve', st[s2][:, 14:15], st[s2][:, 13:14], 1.0, EPS, ALU.mult, ALU.add, [ST], [ST])
                tt('pool', st[s2][:, 15:16], st[s2][:, 14:15], mhalf[:, 0:1], ALU.pow, [ST, 'mhalf'], [ST])
                ts('dve', vh[0][:], vg[s2][:], st[s2][:, 12:13], st[s2][:, 15:16], ALU.subtract, ALU.mult,
                   [f'vg{s2}', ST], ['vh0'])

            def S2b(t):
                if not ok(t):
                    return
                s2, y3 = t % 2, t % 3
                for g in range(8):
                    mm(psM[:, g * 128:(g + 1) * 128], wsT[:, g, :], vh[0][:, g * 128:(g + 1) * 128], True, True,
                       ['wsT', 'vh0'], ['psM'], inc=(g == 7))
                tt('dve', t1[0][:], psM[:], gvb[:], ALU.mult, ['psM', 'gvb'], ['t10'])
                for g in range(8):
                    gs = slice(g * 128, (g + 1) * 128)
                    stt('dve', t1[0][:, gs], t1[0][:, gs], bsT[:, g:g + 1], ug[s2][:, gs], ALU.add, ALU.mult,
                        ['t10', 'bsT', f'ug{s2}'], ['t10'])
                stt('dve', t2[s2][:], ta[s2][:], 1.0, t1[0][:], ALU.add, ALU.mult, [f'ta{s2}', 't10'], [f't2{s2}'])
                stt('dve', ta[s2][:], tb[s2][:], 1.0, ybt[y3][:], ALU.add, ALU.mult, [f'tb{s2}', f'ybt{y3}'],
                    [f'ta{s2}'])
                tt('pool', m2[s2][:], t2[s2][:], ta[s2][:], ALU.add, [f't2{s2}', f'ta{s2}'], [f'm2{s2}'])

            def S3a(t):
                if not ok(t):
                    return
                s2 = t % 2
                for k in range(8):
                    tr(psT[1][:, k, :], m2[s2][:, k * 128:(k + 1) * 128], idb[:], [f'm2{s2}', 'idb'], ['psT1'],
                       inc=(k == 7))
                cp('act', mT[0][:], psT[1][:], ['psT1'], ['mT0'])

            def S3b(t):
                if not ok(t):
                    return
                s2, x4 = t % 2, t % 4
                XIN, XO, SS = f'xin{x4}', f'xo{s2}', f'ss{x4}'
                for n in range(2):
                    for k in range(8):
                        mm(psO[:, n * 512:(n + 1) * 512], mT[0][:, k, :], Wo[:, k, n * 512:(n + 1) * 512], k == 0, k == 7,
                           ['mT0', 'Wo'], ['psO'], inc=(k == 7 and n == 1))
                act(junk[:], psO[:], AF.Square, ['psO'], ['junk', SS], accum=ss[x4][:, 4:5])
                rstd_from_ss(ss[x4][:, 4:5], ss[x4][:, 5:6], ss[x4][:, 6:7], 1, 4 * EPS, SS)
                stt('dve', t2[s2][:], psO[:], ss[x4][:, 6:7], gb_post[:], ALU.mult, ALU.mult,
                    ['psO', SS, 'gb_post'], [f't2{s2}'])
                tt('pool', xo[s2][:], t2[s2][:], xin[x4][:], ALU.add, [f't2{s2}', XIN], [XO])
                dma('sp', x1_scr[t * 128:(t + 1) * 128, :], xo[s2][:], [XO], [], f'd_xo{s2}')

            loads(0)
            loads(1)
            S0a(0)
            S0b(0)
            S1a(0)
            S0a(1)
            S1b(0)
            for j in range(NT + 1):
                loads(j + 2)
                S2a(j)
                S3a(j - 1)
                S0b(j + 1)
                S3b(j - 1)
                S2b(j)
                S1a(j + 1)
                S0a(j + 2)
                S1b(j + 1)
            P.barrier()
        es23.close()

        with contextlib.ExitStack() as es:
            def sbt(name, shape, dt):
                return es.enter_context(nc.sbuf_tensor(f"{name}_{l}", shape, dt))

            def pst(name, shape, dt=F32):
                return es.enter_context(nc.psum_tensor(f"{name}_{l}", shape, dt))

            Wg = sbt("p4Wg", [128, 8, DFF], BF16)
            Wu = sbt("p4Wu", [128, 8, DFF], BF16)
            Wd = sbt("p4Wd", [128, NFC, D], BF16)
            gb_fpre = sbt("p4gbpre", [128, D], F32)
            gb_fpost = sbt("p4gbpost", [128, D], F32)
            xin = [sbt(f"p4xin{i}", [128, D], F32) for i in range(3)]
            xn = [sbt(f"p4xn{i}", [128, D], BF16) for i in range(2)]
            hT = sbt("p4hT", [128, 8, 512], BF16)
            hmT = sbt("p4hmT", [128, NFC, 512], BF16)
            sg = [sbt(f"p4sg{i}", [128, 512], BF16) for i in range(2)]
            ss = sbt("p4ss", [128, 32], F32)
            xo = [sbt(f"p4xo{i}", [128, D], F32) for i in range(2)]
            psG = [pst(f"p4psG{i}", [128, 512]) for i in range(2)]
            psU = [pst(f"p4psU{i}", [128, 512]) for i in range(2)]
            psD = [pst(f"p4psD{i}", [128, D]) for i in range(2)]
            psT = psG[1][:].bitcast(BF16).rearrange("p (k c) -> p k c", k=8)

            FCH = [(0, 2), (2, 6), (6, 12), (12, 18), (18, NFC)]
            FCD = [(0, 6), (6, 12), (12, 18), (18, NFC)]
            gu_chunk = {}
            for ci, (f0, f1) in enumerate(FCH):
                for fc in range(f0, f1):
                    gu_chunk[fc] = ci
                load_w(Wg[:, :, f0 * 128:f1 * 128], w_gate[l, :, f0 * 128:f1 * 128].rearrange("(k p) n -> p k n", p=128),
                       f'Wg#{ci}', f'd_w{2 * ci}')
                load_w(Wu[:, :, f0 * 128:f1 * 128], w_up[l, :, f0 * 128:f1 * 128].rearrange("(k p) n -> p k n", p=128),
                       f'Wu#{ci}', f'd_w{2 * ci + 1}')
            for ci, (f0, f1) in enumerate(FCD):
                load_w(Wd[:, f0:f1, :], w_down[l, f0 * 128:f1 * 128, :].rearrange("(f p) n -> p f n", p=128),
                       f'Wd#{ci}', f'd_w{10 + ci}')
            dma('sp', gb_fpre[:], ffn_pre_g[l].partition_broadcast(128), [], ['gb_fpre'], 'd_g2')
            dma('sp', gb_fpost[:], ffn_post_g[l].partition_broadcast(128), [], ['gb_fpost'], 'd_g3')

            nx = [0]

            def p4_pre(gi, t):
                xs = nx[0] % 3
                nx[0] += 1
                x2s = t % 2
                c = 8 * (t % 2)
                tok0 = gi * 512 + t * 128
                dma('sp', xin[xs][:], x1_scr[tok0:tok0 + 128, :], [], [f'xin{xs}'], f'd_xin{xs}')
                act(junk[:], xin[xs][:], AF.Square, [f'xin{xs}'], ['junk', f'ssA{x2s}'], accum=ss[:, c:c + 1])
                rstd_from_ss(ss[:, c:c + 1], ss[:, c + 1:c + 2], ss[:, c + 2:c + 3], 1, EPS, f'ssA{x2s}')
                stt('dve', xn[x2s][:], xin[xs][:], ss[:, c + 2:c + 3], gb_fpre[:], ALU.mult, ALU.mult,
                    [f'xin{xs}', f'ssA{x2s}', 'gb_fpre'], [f'xn{x2s}'])

            def p4_tr(gi, t):
                x2s = t % 2
                for k in range(8):
                    tr(psT[:, k, :], xn[x2s][:, k * 128:(k + 1) * 128], idb[:], [f'xn{x2s}', 'idb'], ['psG1'],
                       inc=(k == 7))
                cp('act' if t % 2 else 'dve', hT[:, :, t * 128:(t + 1) * 128], psT, ['psG1'], ['hT'])

            def p4_fc(gi):
                for fc in range(NFC):
                    b = fc % 2
                    for k in range(8):
                        mm(psG[b][:], Wg[:, k, fc * 128:(fc + 1) * 128], hT[:, k, :], k == 0, k == 7,
                           [f'Wg#{gu_chunk[fc]}', 'hT'], [f'psG{b}'], inc=(k == 7))
                    for k in range(8):
                        mm(psU[b][:], Wu[:, k, fc * 128:(fc + 1) * 128], hT[:, k, :], k == 0, k == 7,
                           [f'Wu#{gu_chunk[fc]}', 'hT'], [f'psU{b}'], inc=(k == 7))
                    act(sg[b][:], psG[b][:], AF.Silu, [f'psG{b}'], [f'sg{b}'])
                    tt('dve', hmT[:, fc, :], sg[b][:], psU[b][:], ALU.mult, [f'sg{b}', f'psU{b}'], ['hmT'])

            def p4_down(gi, t):
                xs = nx[0] % 3
                nx[0] += 1
                d2 = t % 2
                tok0 = gi * 512 + t * 128
                dma('sp', xin[xs][:], x1_scr[tok0:tok0 + 128, :], [], [f'xin{xs}'], f'd_xin{xs}')
                for n in range(2):
                    for fc in range(NFC):
                        mm(psD[d2][:, n * 512:(n + 1) * 512], hmT[:, fc, t * 128:(t + 1) * 128],
                           Wd[:, fc, n * 512:(n + 1) * 512], fc == 0, fc == NFC - 1, ['hmT', f'Wd#{fc // 6}'],
                           [f'psD{d2}'], inc=(fc == NFC - 1 and n == 1))
                return xs

            def p4_post(gi, t, xs):
                d2 = t % 2
                c = 16 + 8 * d2
                tok0 = gi * 512 + t * 128
                SS = f'ssB{d2}'
                act(junk[:], psD[d2][:], AF.Square, [f'psD{d2}'], ['junk', SS], accum=ss[:, c:c + 1])
                rstd_from_ss(ss[:, c:c + 1], ss[:, c + 1:c + 2], ss[:, c + 2:c + 3], 1, EPS, SS)
                stt('dve', xo[d2][:], psD[d2][:], ss[:, c + 2:c + 3], gb_fpost[:], ALU.mult, ALU.mult,
                    [f'psD{d2}', SS, 'gb_fpost'], [f'xo{d2}'])
                tt('pool', xo[d2][:], xo[d2][:], xin[xs][:], ALU.add, [f'xo{d2}', f'xin{xs}'], [f'xo{d2}'])
                dma('sp', x_dst[tok0:tok0 + 128, :], xo[d2][:], [f'xo{d2}'], [], f'd_xo{d2}')

            for t in range(4):
                p4_pre(0, t)
                p4_tr(0, t)
            for gi in range(NG):
                p4_fc(gi)
                for t in range(4):
                    if gi + 1 < NG:
                        p4_pre(gi + 1, t)
                    xs = p4_down(gi, t)
                    if gi + 1 < NG:
                        p4_tr(gi + 1, t)
                    p4_post(gi, t, xs)
            P.barrier()

    P.emit()
    return nc


_CACHE = {}


def _consts():
    return {
        "c_ident": np.eye(128, dtype=np.float32),
        "c_tril": np.tril(np.ones((128, 128), dtype=np.float32)),
        "c_triu": np.triu(np.ones((128, 128), dtype=np.float32)),
    }


def kernel(x, mix_pre_g, w_in, b_forget, sgu_norm_g, w_spatial, b_spatial, w_out,
           mix_post_g, ffn_pre_g, w_gate, w_up, w_down, ffn_post_g):
    B, S, _ = x.shape
    depth = w_in.shape[0]
    key = (S, depth)
    if key not in _CACHE:
        _CACHE[key] = build_program(S, depth)
    nc = _CACHE[key]
    params = {
        "mix_pre_g": mix_pre_g, "w_in": w_in, "b_forget": b_forget, "sgu_norm_g": sgu_norm_g,
        "w_spatial": w_spatial, "b_spatial": b_spatial, "w_out": w_out, "mix_post_g": mix_post_g,
        "ffn_pre_g": ffn_pre_g, "w_gate": w_gate, "w_up": w_up, "w_down": w_down, "ffn_post_g": ffn_post_g,
    }
    params = {k: np.ascontiguousarray(np.asarray(v, dtype=np.float32)) for k, v in params.items()}
    params.update(_consts())
    xs = np.asarray(x, dtype=np.float32)
    in_maps = []
    for b in range(B):
        m = {"x": np.ascontiguousarray(xs[b])}
        m.update(params)
        in_maps.append(m)
    res = run_bass_kernel_spmd(nc, in_maps, core_ids=list(range(B)))
    return np.stack([np.asarray(r["y"], dtype=np.float32) for r in res.results], axis=0)
```

```python
import contextlib
import numpy as np
import concourse.bass as bass
import concourse.mybir as mybir
from concourse.bass_utils import run_bass_kernel_spmd

F32 = mybir.dt.float32
BF16 = mybir.dt.bfloat16
AF = mybir.ActivationFunctionType
ALU = mybir.AluOpType

D = 1024
NH = 8
DFF = 2816
NFC = DFF // 128
INW = 7176
EPS = 1e-6


class Prog:
    def __init__(self, nc):
        self.nc = nc
        self.engs = ['pe', 'act', 'dve', 'pool', 'sp']
        self.stream = {k: [] for k in self.engs}
        self.sem = {}
        self.cnt = {}
        self.lastw = {}
        self.readers = {}
        self.known = {e: {} for e in self.engs}
        self.clock_at = {}
        for k in self.engs:
            self._mksem(k)

    def _mksem(self, key):
        self.sem[key] = self.nc.alloc_semaphore("s_" + key)
        self.cnt[key] = 0

    def _need(self, e, key, val):
        if e == 'pe' and key == 'pe':
            return
        kn = self.known[e]
        if kn.get(key, 0) >= val:
            return
        sem = self.sem[key]
        self.stream[e].append(lambda E, sem=sem, val=val: E.wait_ge(sem, val))
        kn[key] = val
        snap = self.clock_at.get((key, val))
        if snap:
            for k2, v2 in snap.items():
                if kn.get(k2, 0) < v2:
                    kn[k2] = v2

    def op(self, e, fn, reads=(), writes=(), inc=True, dma=None):
        for r in reads:
            w = self.lastw.get(r)
            if w:
                self._need(e, *w)
        for w_ in writes:
            w = self.lastw.get(w_)
            if w:
                self._need(e, *w)
            for k, v in self.readers.get(w_, {}).items():
                self._need(e, k, v)
        if dma is not None:
            key = dma
            if key not in self.sem:
                self._mksem(key)
            self.cnt[key] += 16
            val = self.cnt[key]
            sem = self.sem[key]
            self.stream[e].append(lambda E, fn=fn, sem=sem: fn(E).then_inc(sem, 16))
            self.clock_at[(key, val)] = dict(self.known[e])
        else:
            key = e
            if inc:
                self.cnt[key] += 1
                val = self.cnt[key]
                sem = self.sem[key]
                self.stream[e].append(lambda E, fn=fn, sem=sem: fn(E).then_inc(sem, 1))
                self.clock_at[(key, val)] = dict(self.known[e])
            else:
                assert e == 'pe'
                val = self.cnt[key] + 1
                self.stream[e].append(lambda E, fn=fn: fn(E))
        for r in reads:
            d = self.readers.setdefault(r, {})
            d[key] = max(d.get(key, 0), val)
        for w_ in writes:
            self.lastw[w_] = (key, val)
            self.readers[w_] = {}

    def barrier(self, engines=None):
        for e in (engines or self.engs):
            for k in list(self.sem.keys()):
                if self.cnt[k] > 0:
                    self._need(e, k, self.cnt[k])

    def emit(self):
        nc = self.nc
        with nc.Block() as block:
            @block.tensor
            def _(E):
                for f in self.stream['pe']:
                    f(E)

            @block.scalar
            def _(E):
                for f in self.stream['act']:
                    f(E)

            @block.vector
            def _(E):
                for f in self.stream['dve']:
                    f(E)

            @block.gpsimd
            def _(E):
                for f in self.stream['pool']:
                    f(E)

            @block.sync
            def _(E):
                for f in self.stream['sp']:
                    f(E)


def build_program(S=4096, depth=2, debug=False):
    NT = S // 128
    NG = S // 512
    nc = bass.Bass("TRN2", target_bir_lowering=False)
    P = Prog(nc)

    def din(name, shape, dt=F32):
        return nc.dram_tensor(name, shape, dt, kind="ExternalInput").ap()

    x_in = din("x", [S, D])
    mix_pre_g = din("mix_pre_g", [depth, D])
    w_in = din("w_in", [depth, D, INW])
    b_forget = din("b_forget", [depth, NH])
    sgu_norm_g = din("sgu_norm_g", [depth, D])
    w_spatial = din("w_spatial", [depth, 8, 128, 128])
    b_spatial = din("b_spatial", [depth, 8, 128])
    w_out = din("w_out", [depth, D, D])
    mix_post_g = din("mix_post_g", [depth, D])
    ffn_pre_g = din("ffn_pre_g", [depth, D])
    w_gate = din("w_gate", [depth, D, DFF])
    w_up = din("w_up", [depth, D, DFF])
    w_down = din("w_down", [depth, DFF, D])
    ffn_post_g = din("ffn_post_g", [depth, D])
    c_ident = din("c_ident", [128, 128])
    c_tril = din("c_tril", [128, 128])
    c_triu = din("c_triu", [128, 128])
    y_out = nc.dram_tensor("y", [S, D], F32, kind="ExternalOutput").ap()

    skind = "ExternalOutput" if debug else "Internal"
    q_scr = nc.dram_tensor("q_scr", [NH, 128, S], BF16, kind=skind).ap()
    k_scr = nc.dram_tensor("k_scr", [NH, 128, S], BF16, kind=skind).ap()
    v_scr = nc.dram_tensor("v_scr", [NH, S, 128], BF16, kind=skind).ap()
    c_scr = nc.dram_tensor("c_scr", [NH, S], F32, kind=skind).ap()
    yb_scr = nc.dram_tensor("yb_scr", [S, D], BF16, kind=skind).ap()
    x1_scr = nc.dram_tensor("x1_scr", [S, D], F32, kind=skind).ap()
    x2_scr = nc.dram_tensor("x2_scr", [S, D], F32, kind=skind).ap()

    def sb(name, shape, dt):
        return nc.alloc_sbuf_tensor(name, shape, dt)

    idf = sb("idf", [128, 128], F32)
    idb = sb("idb", [128, 128], BF16)
    trilf = sb("trilf", [128, 128], F32)
    triuf = sb("triuf", [128, 128], F32)
    negmask = sb("negmask", [128, 128], F32)
    mhalf = sb("mhalf", [128, 8], F32)
    zeros = sb("zeros", [128, 128], F32)
    bsT = sb("bsT", [128, 8], F32)
    negbf = sb("negbf", [8, 1], F32)
    negcT = sb("negcT", [128, NT, 8], F32)
    junk = sb("junk", [128, D], BF16)
    ones8 = sb("ones8", [8, 512], F32)

    def dma(e, out, in_, reads, writes, key, slow=False):
        if slow:
            P.op(e, lambda E: E.dma_start(out=out, in_=in_, allow_slow_non_contiguous=True), reads, writes, dma=key)
        else:
            P.op(e, lambda E: E.dma_start(out=out, in_=in_), reads, writes, dma=key)

    def mm(out, lhsT, rhs, start, stop, reads, writes, inc, skip=False):
        if skip:
            P.op('pe', lambda E: E.matmul(out, lhsT=lhsT, rhs=rhs, start=start, stop=stop, skip_group_check=True),
                 reads, writes, inc=inc)
        else:
            P.op('pe', lambda E: E.matmul(out, lhsT=lhsT, rhs=rhs, start=start, stop=stop), reads, writes, inc=inc)

    def tr(out, in_, ident, reads, writes, inc):
        P.op('pe', lambda E: E.transpose(out=out, in_=in_, identity=ident), reads, writes, inc=inc)

    def act(out, in_, func, reads, writes, bias=None, scale=1.0, accum=None):
        kw = {}
        if bias is not None:
            kw['bias'] = bias
        if accum is not None:
            kw['accum_out'] = accum
        P.op('act', lambda E: E.activation(out=out, in_=in_, func=func, scale=scale, **kw), reads, writes)

    def tt(e, out, in0, in1, op, reads, writes):
        P.op(e, lambda E: E.tensor_tensor(out=out, in0=in0, in1=in1, op=op), reads, writes)

    def ts(e, out, in0, s1, s2, op0, op1, reads, writes):
        if s2 is None:
            P.op(e, lambda E: E.tensor_scalar(out=out, in0=in0, scalar1=s1, scalar2=None, op0=op0), reads, writes)
        else:
            P.op(e, lambda E: E.tensor_scalar(out=out, in0=in0, scalar1=s1, scalar2=s2, op0=op0, op1=op1), reads, writes)

    def stt(e, out, in0, scalar, in1, op0, op1, reads, writes):
        P.op(e, lambda E: E.scalar_tensor_tensor(out=out, in0=in0, scalar=scalar, in1=in1, op0=op0, op1=op1),
             reads, writes)

    def cp(e, out, in_, reads, writes):
        if e == 'act':
            act(out, in_, AF.Copy, reads, writes)
        else:
            P.op(e, lambda E: E.tensor_copy(out=out, in_=in_), reads, writes)

    def memset(e, ap, val, writes):
        P.op(e, lambda E: E.memset(ap, val), (), writes)

    def rstd_from_ss(ss_ap, ms_ap, out_ap, n, eps, res):
        ts('dve', ms_ap, ss_ap, 1.0 / D, eps, ALU.mult, ALU.add, [res], [res])
        tt('pool', out_ap, ms_ap, mhalf[:, 0:n], ALU.pow, [res, 'mhalf'], [res])

    dma('sp', idf[:], c_ident, [], ['idf'], 'd_c0')
    dma('sp', trilf[:], c_tril, [], ['trilf'], 'd_c1')
    dma('sp', triuf[:], c_triu, [], ['triuf'], 'd_c2')
    cp('dve', idb[:], idf[:], ['idf'], ['idb'])
    ts('dve', negmask[:], triuf[:], -1.0, 30000.0, ALU.add, ALU.mult, ['triuf'], ['negmask'])
    memset('pool', mhalf[:], -0.5, ['mhalf'])
    memset('pool', zeros[:], 0.0, ['zeros'])
    memset('pool', ones8[:], 1.0, ['ones8'])

    def load_w(dst_ap, src_ap, res, key, slow=False):
        dma('pool', dst_ap, src_ap, [], [res], key, slow=slow)

    for l in range(depth):
        x_src = x_in if l == 0 else x2_scr
        x_dst = y_out if l == depth - 1 else x2_scr

        with contextlib.ExitStack() as es:
            def sbt(name, shape, dt):
                return es.enter_context(nc.sbuf_tensor(f"{name}_{l}", shape, dt))

            def pst(name, shape, dt=F32):
                return es.enter_context(nc.psum_tensor(f"{name}_{l}", shape, dt))

            W1 = sbt("p1W1", [128, 8, 3080], BF16)
            gb_pre = sbt("p1gbpre", [128, D], F32)
            xin = [sbt(f"p1xin{i}", [128, 4, D], F32) for i in range(2)]
            xn = sbt("p1xn", [128, 4, D], BF16)
            hT = [sbt(f"p1hT{i}", [128, 8, 512], BF16) for i in range(2)]
            ss = sbt("p1ss", [128, 16], F32)
            qk_sb = [sbt(f"p1qk{i}", [128, 512], BF16) for i in range(4)]
            v_sb = [sbt(f"p1v{i}", [128, D], BF16) for i in range(2)]
            ef_t = sbt("p1ef", [128, 512], F32)
            lf_t = sbt("p1lf", [128, 512], F32)
            cg_t = [sbt(f"p1cg{i}", [128, 512], F32) for i in range(2)]
            dma('sp', ef_t[:], x_in[0:128, 0:512], [], ['ef'], 'd_i0')
            dma('sp', lf_t[:], x_in[0:128, 0:512], [], ['lf'], 'd_i1')
            dma('sp', cg_t[0][:], x_in[0:128, 0:512], [], ['cg0'], 'd_i2')
            dma('sp', cg_t[1][:], x_in[0:128, 0:512], [], ['cg1'], 'd_i3')
            ef = ef_t[0:8, :]
            lf = lf_t[0:8, :]
            cg = [cg_t[0][0:8, :], cg_t[1][0:8, :]]
            psT = [pst(f"p1psT{i}", [128, 8, 128], BF16) for i in range(2)]
            psA = [pst(f"p1psA{i}", [128, 512]) for i in range(4)]
            psF = pst("p1psF", [8, 512])
            psC = pst("p1psC", [128, 4, 8])

            load_w(W1[:, :, 3072:3080], w_in[l, :, 7168:INW].rearrange("(k p) n -> p k n", p=128), 'W1c6', 'd_w6',
                   slow=True)
            for ci in range(6):
                load_w(W1[:, :, ci * 512:(ci + 1) * 512],
                       w_in[l, :, 2048 + ci * 512:2048 + (ci + 1) * 512].rearrange("(k p) n -> p k n", p=128),
                       f'W1c{ci}', f'd_w{ci}')
            dma('sp', gb_pre[:], mix_pre_g[l].partition_broadcast(128), [], ['gb_pre'], 'd_g0')
            dma('sp', negbf[:], b_forget[l].rearrange("(h o) -> h o", o=1), [], ['negbf'], 'd_g6', slow=True)
            ts('dve', negbf[:], negbf[:], -1.0, None, ALU.mult, None, ['negbf'], ['negbf'])

            na = [0]

            def p1_load(gi):
                s2 = gi % 2
                dma('sp', xin[s2][:], x_src[gi * 512:(gi + 1) * 512, :].rearrange("(t p) d -> p t d", p=128),
                    [], [f'xin{s2}'], f'd_xin{s2}')

            def p1_stats(gi):
                s2 = gi % 2
                XIN = f'xin{s2}'
                for t in range(4):
                    act(junk[:], xin[s2][:, t, :], AF.Square, [XIN], ['junk', 'ss'], accum=ss[:, t:t + 1])
                rstd_from_ss(ss[:, 0:4], ss[:, 4:8], ss[:, 8:12], 4, EPS, 'ss')
                for t in range(4):
                    stt('dve', xn[:, t, :], xin[s2][:, t, :], ss[:, 8 + t:9 + t], gb_pre[:], ALU.mult, ALU.mult,
                        [XIN, 'ss', 'gb_pre'], [f'xn{t}'])

            def p1_tr(gi):
                s2 = gi % 2
                HT = f'hT{s2}'
                for t in range(4):
                    pt = t % 2
                    for k in range(8):
                        tr(psT[pt][:, k, :], xn[:, t, k * 128:(k + 1) * 128], idb[:], [f'xn{t}', 'idb'], [f'psT{pt}'],
                           inc=(k == 7))
                    cp('act' if t % 2 else 'dve', hT[s2][:, :, t * 128:(t + 1) * 128], psT[pt][:],
                       [f'psT{pt}'], [HT])

            def p1_qk(gi, n0, n1):
                s2 = gi % 2
                HT = f'hT{s2}'
                for n in range(n0, n1):
                    b = na[0] % 4
                    na[0] += 1
                    for k in range(8):
                        mm(psA[b][:], W1[:, k, n * 128:(n + 1) * 128], hT[s2][:, k, :], k == 0, k == 7,
                           [f'W1c{n // 4}', HT], [f'psA{b}'], inc=(k == 7))
                    if n < 8:
                        act(qk_sb[b][:], psA[b][:], AF.Copy, [f'psA{b}'], [f'qk{b}'], scale=float(128 ** -0.5))
                        dma('sp', q_scr[n, :, gi * 512:(gi + 1) * 512], qk_sb[b][:], [f'qk{b}'], [], f'd_qk{b}')
                    else:
                        cp('dve', qk_sb[b][:], psA[b][:], [f'psA{b}'], [f'qk{b}'])
                        dma('sp', k_scr[n - 8, :, gi * 512:(gi + 1) * 512], qk_sb[b][:], [f'qk{b}'], [],
                            f'd_qk{b}')

            def p1_v(gi):
                s2 = gi % 2
                HT = f'hT{s2}'
                for t in range(4):
                    vs = t % 2
                    for n2 in range(2):
                        b = na[0] % 4
                        na[0] += 1
                        for k in range(8):
                            mm(psA[b][:], hT[s2][:, k, t * 128:(t + 1) * 128],
                               W1[:, k, 2048 + n2 * 512:2048 + (n2 + 1) * 512], k == 0, k == 7,
                               [f'W1c{4 + n2}', HT], [f'psA{b}'], inc=(k == 7))
                        cp('act' if n2 else 'dve', v_sb[vs][:, n2 * 512:(n2 + 1) * 512], psA[b][:],
                           [f'psA{b}'], [f'v{vs}'])
                    tok0 = gi * 512 + t * 128
                    dma('sp', v_scr[:, tok0:tok0 + 128, :].rearrange("h t d -> t h d"),
                        v_sb[vs][:].rearrange("p (h d) -> p h d", h=8), [f'v{vs}'], [], f'd_v{vs}')

            def p1_f(gi):
                s2 = gi % 2
                HT = f'hT{s2}'
                for k in range(8):
                    mm(psF[:], W1[:, k, 3072:3080], hT[s2][:, k, :], k == 0, k == 7, ['W1c6', HT], ['psF'],
                       inc=(k == 7))
                act(ef, psF[:], AF.Exp, ['psF', 'negbf'], ['ef'], bias=negbf[:, 0:1], scale=-1.0)
                act(lf, ef, AF.Ln, ['ef'], ['lf'], bias=1.0)
                init = 0.0 if gi == 0 else cg[1 - s2][:, 511:512]
                P.op('dve', lambda E, o=cg[s2], i=init, lf_=lf, on_=ones8[:]: E.tensor_tensor_scan(
                    out=o, data0=on_, data1=lf_, initial=i, op0=ALU.mult, op1=ALU.subtract),
                    ['ones8', 'lf', f'cg{1 - s2}'], [f'cg{s2}'])
                dma('sp', c_scr[:, gi * 512:(gi + 1) * 512], cg[s2], [f'cg{s2}'], [], f'd_cg{s2}')

            def p1_cgT(gi):
                s2 = gi % 2
                for j in range(4):
                    tr(psC[:, j, :], cg[s2][:, j * 128:(j + 1) * 128], idf[0:8, 0:8], [f'cg{s2}', 'idf'], ['psC'],
                       inc=(j == 3))
                ts('dve', negcT[:, gi * 4:(gi + 1) * 4, :], psC[:], -1.0, None, ALU.mult, None, ['psC'], ['negcT'])

            p1_load(0)
            p1_stats(0)
            p1_tr(0)
            for gi in range(NG):
                if gi + 1 < NG:
                    p1_load(gi + 1)
                p1_f(gi)
                p1_qk(gi, 0, 6)
                if gi + 1 < NG:
                    p1_stats(gi + 1)
                p1_qk(gi, 6, 16)
                if gi > 0:
                    p1_cgT(gi - 1)
                if gi + 1 < NG:
                    p1_tr(gi + 1)
                p1_v(gi)
            p1_cgT(NG - 1)
            P.barrier()

        es23 = contextlib.ExitStack()
        es23.__enter__()

        def sbt23(name, shape, dt):
            return es23.enter_context(nc.sbuf_tensor(f"{name}_{l}", shape, dt))

        W3 = sbt23("p3W3", [128, 8, 4096], BF16)
        Wo = sbt23("p3Wo", [128, 8, D], BF16)
        gb_pre3 = sbt23("p3gbpre", [128, D], F32)
        gb_post = sbt23("p3gbpost", [128, D], F32)
        gvb = sbt23("p3gvb", [128, D], F32)

        def load_p3_weights():
            for n in range(8):
                c_src = n * 512 if n < 4 else 5120 + (n - 4) * 512
                load_w(W3[:, :, n * 512:(n + 1) * 512],
                       w_in[l, :, c_src:c_src + 512].rearrange("(k p) n -> p k n", p=128), f'W3c{n}', f'd_w{n}')
            load_w(Wo[:], w_out[l].rearrange("(k p) n -> p k n", p=128), 'Wo', 'd_w8')
            dma('sp', gb_pre3[:], mix_pre_g[l].partition_broadcast(128), [], ['gb_pre3'], 'd_g0')
            dma('sp', gb_post[:], mix_post_g[l].partition_broadcast(128), [], ['gb_post'], 'd_g1')
            dma('sp', gvb[:], sgu_norm_g[l].partition_broadcast(128), [], ['gvb'], 'd_g4')

        with contextlib.ExitStack() as es:
            def sbt(name, shape, dt):
                return es.enter_context(nc.sbuf_tensor(f"{name}_{l}", shape, dt))

            def pst(name, shape, dt=F32):
                return es.enter_context(nc.psum_tensor(f"{name}_{l}", shape, dt))

            qT = [sbt(f"p2q{i}", [128, S], BF16) for i in range(2)]
            kT = [sbt(f"p2k{i}", [128, S], BF16) for i in range(2)]
            V = [sbt(f"p2v{i}", [128, NT, 129], BF16) for i in range(2)]
            cbc = [sbt(f"p2c{i}", [128, 1024], F32) for i in range(3)]
            cbcD = [sbt(f"p2cD{i}", [128, 1024], F32) for i in range(3)]
            sadd = [sbt(f"p2sa{i}", [128, 1024], F32) for i in range(4)]
            pT = [sbt(f"p2pT{i}", [128, 1024], BF16) for i in range(3)]
            yb_sb = [sbt(f"p2yb{i}", [128, 4, 128], BF16) for i in range(4)]
            rinv = sbt("p2rinv", [128, 8], F32)
            psS = [pst(f"p2psS{i}", [128, 1024]) for i in range(2)]
            po = [pst(f"p2po{i}", [128, 2, 256]) for i in range(4)]

            for s2 in range(2):
                memset('pool', V[s2][:, :, 128:129], 1.0, [f'V{s2}'])

            def load_head(h):
                s2 = h % 2
                dma('sp', qT[s2][:], q_scr[h], [], [f'qT{s2}'], f'd_q{s2}')
                dma('sp', kT[s2][:], k_scr[h], [], [f'kT{s2}'], f'd_k{s2}')
                dma('sp', V[s2][:, :, 0:128], v_scr[h].rearrange("(kb p) d -> p kb d", p=128), [],
                    [f'V{s2}'], f'd_vv{s2}')

            def load_cbc(rd):
                h, jp = rd // (NG // 2), rd % (NG // 2)
                c3 = rd % 3
                dma('sp', cbc[c3][:], c_scr[h, jp * 1024:(jp + 1) * 1024].partition_broadcast(128), [],
                    [f'cbc{c3}'], f'd_c{c3}')
                for tix in range(8):
                    tt('pool', cbcD[c3][:, tix * 128:(tix + 1) * 128], cbc[c3][:, tix * 128:(tix + 1) * 128], negmask[:],
                       ALU.add, [f'cbc{c3}', 'negmask'], [f'cbcD{c3}'])

            steps = []
            for h in range(NH):
                for jp in range(NG // 2):
                    j0, j1 = 2 * jp, 2 * jp + 1
                    for kb in range(4 * j1 + 4):
                        steps.append((h, j0, j1, kb, kb < 4 * j0 + 4))
            NS = len(steps)

            def geom(n):
                h, j0, j1, kb, has0 = steps[n]
                r0 = kb - 4 * j0
                r1 = kb - 4 * j1
                c0 = 128 * max(r0, 0)
                c1 = 128 * max(r1, 0)
                lo = c0 if has0 else 512 + c1
                return h, j0, j1, kb, has0, r0, r1, c0, c1, lo

            def emit_qk(n):
                h, j0, j1, kb, has0, r0, r1, c0, c1, lo = geom(n)
                s2 = h % 2
                b = n % 2
                ks = kT[s2][:, kb * 128:(kb + 1) * 128]
                if has0:
                    mm(psS[b][:, c0:512], ks, qT[s2][:, j0 * 512 + c0:(j0 + 1) * 512],
                       True, True, [f'kT{s2}', f'qT{s2}'], [f'psS{b}'], inc=False)
                mm(psS[b][:, 512 + c1:1024], ks, qT[s2][:, j1 * 512 + c1:(j1 + 1) * 512],
                   True, True, [f'kT{s2}', f'qT{s2}'], [f'psS{b}'], inc=True)

            def emit_add(n):
                h, j0, j1, kb, has0, r0, r1, c0, c1, lo = geom(n)
                b = n % 2
                a4 = n % 4
                c3 = (h * (NG // 2) + j0 // 2) % 3
                d0 = None
                if has0 and r0 >= 0:
                    d0 = c0
                elif r1 >= 0:
                    d0 = 512 + c1
                if d0 is not None:
                    assert d0 == lo
                    tt('dve', sadd[a4][:, d0:d0 + 128], psS[b][:, d0:d0 + 128], cbcD[c3][:, d0:d0 + 128], ALU.add,
                       [f'psS{b}', f'cbcD{c3}'], [f'sadd{a4}'])
                    lo2 = d0 + 128
                else:
                    lo2 = lo
                if lo2 < 1024:
                    tt('dve', sadd[a4][:, lo2:1024], psS[b][:, lo2:1024], cbc[c3][:, lo2:1024], ALU.add,
                       [f'psS{b}', f'cbc{c3}'], [f'sadd{a4}'])

            def emit_tail(n):
                h, j0, j1, kb, has0, r0, r1, c0, c1, lo = geom(n)
                s2 = h % 2
                a4 = n % 4
                a3 = n % 3
                act(pT[a3][:, lo:1024], sadd[a4][:, lo:1024], AF.Exp, [f'sadd{a4}', 'negcT'], [f'pT{a3}'],
                    bias=negcT[:, kb, h:h + 1])
                tiles = []
                if has0:
                    tiles += [(0, i) for i in range(max(r0, 0), 4)]
                tiles += [(1, i) for i in range(max(r1, 0), 4)]
                for idx, (jj, i) in enumerate(tiles):
                    u = 4 * jj + i
                    bank, slot = u // 2, u % 2
                    first = (kb == 0 and slot == 0)
                    mm(po[bank][:, slot, 0:129], pT[a3][:, jj * 512 + i * 128:jj * 512 + (i + 1) * 128], V[s2][:, kb, :],
                       first, False, [f'pT{a3}', f'V{s2}'], [f'po{bank}'], inc=(idx == len(tiles) - 1), skip=True)

            def emit_evac(n):
                h, j0, j1, kb, has0, r0, r1, c0, c1, lo = geom(n)
                done = []
                if has0 and r0 in (1, 3):
                    done.append((0, r0, j0))
                if r1 in (1, 3):
                    done.append((1, r1, j1))
                for jj, r, j in done:
                    ys = (h * NG + j) % 4
                    for i in (r - 1, r):
                        u = 4 * jj + i
                        bank, slot = u // 2, u % 2
                        P.op('dve', lambda E, o_=rinv[:, u:u + 1], i_=po[bank][:, slot, 128:129]: E.reciprocal(
                            out=o_, in_=i_), [f'po{bank}'], [f'rinv{u}'])
                        ts('dve', yb_sb[ys][:, i, :], po[bank][:, slot, 0:128], rinv[:, u:u + 1], None, ALU.mult, None,
                           [f'po{bank}', f'rinv{u}'], [f'yb{ys}'])
                    if r == 3:
                        dma('sp', yb_scr[j * 512:(j + 1) * 512, h * 128:(h + 1) * 128].rearrange("(i p) d -> p i d", p=128),
                            yb_sb[ys][:], [f'yb{ys}'], [], f'd_yb{ys}')

            def round_of(n):
                h, j0 = steps[n][0], steps[n][1]
                return h * (NG // 2) + j0 // 2

            NR = NH * (NG // 2)
            AHEAD = 2
            load_head(0)
            load_cbc(0)
            if NR > 1:
                load_cbc(1)
            load_p3_weights()
            cur_head, cur_round = 0, 1
            for n in range(min(AHEAD, NS)):
                emit_qk(n)
                emit_add(n)
            for n in range(NS):
                h = steps[n][0]
                if h + 1 < NH and cur_head < h + 1:
                    load_head(h + 1)
                    cur_head = h + 1
                rd = round_of(n)
                if rd + 2 < NR and cur_round < rd + 2:
                    load_cbc(rd + 2)
                    cur_round = rd + 2
                if n + AHEAD < NS:
                    emit_qk(n + AHEAD)
                    emit_add(n + AHEAD)
                new_round = (steps[n][3] == 0)
                if n >= 1 and new_round:
                    emit_evac(n - 1)
                emit_tail(n)
                if n >= 1 and not new_round:
                    emit_evac(n - 1)
            emit_evac(NS - 1)
            P.barrier()

        with contextlib.ExitStack() as es:
            def sbt(name, shape, dt):
                return es.enter_context(nc.sbuf_tensor(f"{name}_{l}", shape, dt))

            def pst(name, shape, dt=F32):
                return es.enter_context(nc.psum_tensor(f"{name}_{l}", shape, dt))

            wsT = sbt("p3wsT", [128, 8, 128], BF16)
            xin = [sbt(f"p3xin{i}", [128, D], F32) for i in range(4)]
            ybt = [sbt(f"p3yb{i}", [128, D], BF16) for i in range(3)]
            xn = [sbt(f"p3xn{i}", [128, D], BF16) for i in range(2)]
            hT = [sbt(f"p3hT{i}", [128, 8, 128], BF16) for i in range(2)]
            ss = [sbt(f"p3ss{i}", [128, 16], F32) for i in range(4)]
            st = [sbt(f"p3st{i}", [128, 16], F32) for i in range(2)]
            ug = [sbt(f"p3ug{i}", [128, D], F32) for i in range(2)]
            vg = [sbt(f"p3vg{i}", [128, D], F32) for i in range(2)]
            vh = [sbt(f"p3vh{i}", [128, D], BF16) for i in range(1)]
            ta = [sbt(f"p3ta{i}", [128, D], F32) for i in range(2)]
            tb = [sbt(f"p3tb{i}", [128, D], F32) for i in range(2)]
            t1 = [sbt(f"p3t1{i}", [128, D], F32) for i in range(1)]
            t2 = [sbt(f"p3t2{i}", [128, D], F32) for i in range(2)]
            m2 = [sbt(f"p3m2{i}", [128, D], BF16) for i in range(2)]
            mT = [sbt(f"p3mT{i}", [128, 8, 128], BF16) for i in range(1)]
            xo = [sbt(f"p3xo{i}", [128, D], F32) for i in range(2)]
            wsf = t2[0][:].rearrange("p (g s) -> p g s", g=8)
            wsm = m2[0][:].rearrange("p (g s) -> p g s", g=8)
            psT = [pst(f"p3psT{i}", [128, 8, 128], BF16) for i in range(2)]
            psP = [pst(f"p3psP{i}", [128, 512]) for i in range(2)]
            psM = pst("p3psM", [128, D])
            psO = pst("p3psO", [128, D])

            dma('sp', bsT[:], b_spatial[l].rearrange("g t -> t g"), [], ['bsT'], 'd_g5', slow=True)
            dma('sp', wsf, w_spatial[l].rearrange("g t s -> t g s"), [], ['t20'], 'd_g7')
            for g in range(8):
                tt('dve', wsm[:, g, :], wsf[:, g, :], trilf[:], ALU.mult, ['t20', 'trilf'], ['m20'])
            for g in range(8):
                tr(psT[1][:, g, :], wsm[:, g, :], idb[:], ['m20', 'idb'], ['psT1'], inc=(g == 7))
            cp('dve', wsT[:], psT[1][:], ['psT1'], ['wsT'])

            pcount = [0]

            def proj(ti, n):
                s2 = ti % 2
                b = pcount[0] % 2
                pcount[0] += 1
                for k in range(8):
                    mm(psP[b][:], hT[s2][:, k, :], W3[:, k, n * 512:(n + 1) * 512], k == 0, k == 7,
                       [f'hT{s2}', f'W3c{n}'], [f'psP{b}'], inc=(k == 7))
                cs = slice((n % 2) * 512, (n % 2 + 1) * 512)
                if n < 2:
                    act(ug[s2][:, cs], psP[b][:], AF.Gelu_apprx_tanh, [f'psP{b}'], [f'ug{s2}'])
                elif n < 4:
                    act(vg[s2][:, cs], psP[b][:], AF.Gelu_apprx_tanh, [f'psP{b}'], [f'vg{s2}'])
                elif n < 6:
                    act(ta[s2][:, cs], psP[b][:], AF.Tanh, [f'psP{b}'], [f'ta{s2}'], scale=0.5)
                else:
                    act(tb[s2][:, cs], psP[b][:], AF.Tanh, [f'psP{b}'], [f'tb{s2}'], scale=0.5)

            def ok(t):
                return 0 <= t < NT

            def loads(t):
                if not ok(t):
                    return
                x4, y3 = t % 4, t % 3
                dma('sp', xin[x4][:], x_src[t * 128:(t + 1) * 128, :], [], [f'xin{x4}'], f'd_xin{x4}')
                dma('sp', ybt[y3][:], yb_scr[t * 128:(t + 1) * 128, :], [], [f'ybt{y3}'], f'd_ybt{y3}')

            def S0a(t):
                if not ok(t):
                    return
                x4, s2 = t % 4, t % 2
                XIN, SS = f'xin{x4}', f'ss{x4}'
                act(junk[:], xin[x4][:], AF.Square, [XIN], ['junk', SS], accum=ss[x4][:, 0:1])
                rstd_from_ss(ss[x4][:, 0:1], ss[x4][:, 1:2], ss[x4][:, 2:3], 1, EPS, SS)
                stt('dve', xn[s2][:], xin[x4][:], ss[x4][:, 2:3], gb_pre3[:], ALU.mult, ALU.mult,
                    [XIN, SS, 'gb_pre3'], [f'xn{s2}'])

            def S0b(t):
                if not ok(t):
                    return
                s2 = t % 2
                for k in range(8):
                    tr(psT[0][:, k, :], xn[s2][:, k * 128:(k + 1) * 128], idb[:], [f'xn{s2}', 'idb'], ['psT0'],
                       inc=(k == 7))
                cp('act', hT[s2][:], psT[0][:], ['psT0'], [f'hT{s2}'])

            def S1a(t):
                if ok(t):
                    for n in range(4):
                        proj(t, n)

            def S1b(t):
                if ok(t):
                    for n in range(4, 8):
                        proj(t, n)

            def S2a(t):
                if not ok(t):
                    return
                s2 = t % 2
                ST = f'st{s2}'
                P.op('dve', lambda E, o_=st[s2][:, 0:6], i_=vg[s2][:, 0:512]: E.bn_stats(out=o_, in_=i_),
                     [f'vg{s2}'], [ST])
                P.op('dve', lambda E, o_=st[s2][:, 6:12], i_=vg[s2][:, 512:1024]: E.bn_stats(out=o_, in_=i_),
                     [f'vg{s2}'], [ST])
                P.op('dve', lambda E, o_=st[s2][:, 12:14], i_=st[s2][:, 0:12]: E.bn_aggr(out=o_, in_=i_), [ST], [ST])
                ts('dve', st[s2][:, 14:15], st[s2][:, 13:14], 1.0, EPS, ALU.mult, ALU.add, [ST], [ST])
                tt('pool', st[s2][:, 15:16], st[s2][:, 14:15], mhalf[:, 0:1], ALU.pow, [ST, 'mhalf'], [ST])
                ts('dve', vh[0][:], vg[s2][:], st[s2][:, 12:13], st[s2][:, 15:16], ALU.subtract, ALU.mult,
                   [f'vg{s2}', ST], ['vh0'])

            def S2b(t):
                if not ok(t):
                    return
                s2, y3 = t % 2, t % 3
                for g in range(8):
                    mm(psM[:, g * 128:(g + 1) * 128], wsT[:, g, :], vh[0][:, g * 128:(g + 1) * 128], True, True,
                       ['wsT', 'vh0'], ['psM'], inc=(g == 7))
                tt('dve', t1[0][:], psM[:], gvb[:], ALU.mult, ['psM', 'gvb'], ['t10'])
                for g in range(8):
                    gs = slice(g * 128, (g + 1) * 128)
                    stt('dve', t1[0][:, gs], t1[0][:, gs], bsT[:, g:g + 1], ug[s2][:, gs], ALU.add, ALU.mult,
                        ['t10', 'bsT', f'ug{s2}'], ['t10'])
                stt('dve', t2[s2][:], ta[s2][:], 1.0, t1[0][:], ALU.add, ALU.mult, [f'ta{s2}', 't10'], [f't2{s2}'])
                stt('dve', ta[s2][:], tb[s2][:], 1.0, ybt[y3][:], ALU.add, ALU.mult, [f'tb{s2}', f'ybt{y3}'],
                    [f'ta{s2}'])
                tt('pool', m2[s2][:], t2[s2][:], ta[s2][:], ALU.add, [f't2{s2}', f'ta{s2}'], [f'm2{s2}'])

            def S3a(t):
                if not ok(t):
                    return
                s2 = t % 2
                for k in range(8):
                    tr(psT[1][:, k, :], m2[s2][:, k * 128:(k + 1) * 128], idb[:], [f'm2{s2}', 'idb'], ['psT1'],
                       inc=(k == 7))
                cp('act', mT[0][:], psT[1][:], ['psT1'], ['mT0'])

            def S3b(t):
                if not ok(t):
                    return
                s2, x4 = t % 2, t % 4
                XIN, XO, SS = f'xin{x4}', f'xo{s2}', f'ss{x4}'
                for n in range(2):
                    for k in range(8):
                        mm(psO[:, n * 512:(n + 1) * 512], mT[0][:, k, :], Wo[:, k, n * 512:(n + 1) * 512], k == 0, k == 7,
                           ['mT0', 'Wo'], ['psO'], inc=(k == 7 and n == 1))
                act(junk[:], psO[:], AF.Square, ['psO'], ['junk', SS], accum=ss[x4][:, 4:5])
                rstd_from_ss(ss[x4][:, 4:5], ss[x4][:, 5:6], ss[x4][:, 6:7], 1, 4 * EPS, SS)
                stt('dve', t2[s2][:], psO[:], ss[x4][:, 6:7], gb_post[:], ALU.mult, ALU.mult,
                    ['psO', SS, 'gb_post'], [f't2{s2}'])
                tt('pool', xo[s2][:], t2[s2][:], xin[x4][:], ALU.add, [f't2{s2}', XIN], [XO])
                dma('sp', x1_scr[t * 128:(t + 1) * 128, :], xo[s2][:], [XO], [], f'd_xo{s2}')

            loads(0)
            loads(1)
            S0a(0)
            S0b(0)
            S1a(0)
            S0a(1)
            S1b(0)
            for j in range(NT + 1):
                loads(j + 2)
                S2a(j)
                S3a(j - 1)
                S0b(j + 1)
                S3b(j - 1)
                S2b(j)
                S1a(j + 1)
                S0a(j + 2)
                S1b(j + 1)
            P.barrier()
        es23.close()

        with contextlib.ExitStack() as es:
            def sbt(name, shape, dt):
                return es.enter_context(nc.sbuf_tensor(f"{name}_{l}", shape, dt))

            def pst(name, shape, dt=F32):
                return es.enter_context(nc.psum_tensor(f"{name}_{l}", shape, dt))

            Wg = sbt("p4Wg", [128, 8, DFF], BF16)
            Wu = sbt("p4Wu", [128, 8, DFF], BF16)
            Wd = sbt("p4Wd", [128, NFC, D], BF16)
            gb_fpre = sbt("p4gbpre", [128, D], F32)
            gb_fpost = sbt("p4gbpost", [128, D], F32)
            xin = [sbt(f"p4xin{i}", [128, D], F32) for i in range(3)]
            xn = [sbt(f"p4xn{i}", [128, D], BF16) for i in range(2)]
            hT = sbt("p4hT", [128, 8, 512], BF16)
            hmT = sbt("p4hmT", [128, NFC, 512], BF16)
            sg = [sbt(f"p4sg{i}", [128, 512], BF16) for i in range(2)]
            ss = sbt("p4ss", [128, 32], F32)
            xo = [sbt(f"p4xo{i}", [128, D], F32) for i in range(2)]
            psG = [pst(f"p4psG{i}", [128, 512]) for i in range(2)]
            psU = [pst(f"p4psU{i}", [128, 512]) for i in range(2)]
            psD = [pst(f"p4psD{i}", [128, D]) for i in range(2)]
            psT = psG[1][:].bitcast(BF16).rearrange("p (k c) -> p k c", k=8)

            FCH = [(0, 2), (2, 6), (6, 12), (12, 18), (18, NFC)]
            FCD = [(0, 6), (6, 12), (12, 18), (18, NFC)]
            gu_chunk = {}
            for ci, (f0, f1) in enumerate(FCH):
                for fc in range(f0, f1):
                    gu_chunk[fc] = ci

            def load_gu(ci):
                f0, f1 = FCH[ci]
                load_w(Wg[:, :, f0 * 128:f1 * 128], w_gate[l, :, f0 * 128:f1 * 128].rearrange("(k p) n -> p k n", p=128),
                       f'Wg#{ci}', f'd_w{2 * ci}')
                load_w(Wu[:, :, f0 * 128:f1 * 128], w_up[l, :, f0 * 128:f1 * 128].rearrange("(k p) n -> p k n", p=128),
                       f'Wu#{ci}', f'd_w{2 * ci + 1}')

            def load_d(ci):
                f0, f1 = FCD[ci]
                load_w(Wd[:, f0:f1, :], w_down[l, f0 * 128:f1 * 128, :].rearrange("(f p) n -> p f n", p=128),
                       f'Wd#{ci}', f'd_w{10 + ci}')

            load_gu(0)
            load_gu(1)
            dma('sp', gb_fpre[:], ffn_pre_g[l].partition_broadcast(128), [], ['gb_fpre'], 'd_g2')
            dma('sp', gb_fpost[:], ffn_post_g[l].partition_broadcast(128), [], ['gb_fpost'], 'd_g3')

            nx = [0]

            def p4_pre(gi, t):
                xs = nx[0] % 3
                nx[0] += 1
                x2s = t % 2
                c = 8 * (t % 2)
                tok0 = gi * 512 + t * 128
                dma('sp', xin[xs][:], x1_scr[tok0:tok0 + 128, :], [], [f'xin{xs}'], f'd_xin{xs}')
                act(junk[:], xin[xs][:], AF.Square, [f'xin{xs}'], ['junk', f'ssA{x2s}'], accum=ss[:, c:c + 1])
                rstd_from_ss(ss[:, c:c + 1], ss[:, c + 1:c + 2], ss[:, c + 2:c + 3], 1, EPS, f'ssA{x2s}')
                stt('dve', xn[x2s][:], xin[xs][:], ss[:, c + 2:c + 3], gb_fpre[:], ALU.mult, ALU.mult,
                    [f'xin{xs}', f'ssA{x2s}', 'gb_fpre'], [f'xn{x2s}'])

            def p4_tr(gi, t):
                x2s = t % 2
                for k in range(8):
                    tr(psT[:, k, :], xn[x2s][:, k * 128:(k + 1) * 128], idb[:], [f'xn{x2s}', 'idb'], ['psG1'],
                       inc=(k == 7))
                cp('act' if t % 2 else 'dve', hT[:, :, t * 128:(t + 1) * 128], psT, ['psG1'], ['hT'])

            def p4_fc(gi):
                for fc in range(NFC):
                    b = fc % 2
                    for k in range(8):
                        mm(psG[b][:], Wg[:, k, fc * 128:(fc + 1) * 128], hT[:, k, :], k == 0, k == 7,
                           [f'Wg#{gu_chunk[fc]}', 'hT'], [f'psG{b}'], inc=(k == 7))
                    for k in range(8):
                        mm(psU[b][:], Wu[:, k, fc * 128:(fc + 1) * 128], hT[:, k, :], k == 0, k == 7,
                           [f'Wu#{gu_chunk[fc]}', 'hT'], [f'psU{b}'], inc=(k == 7))
                    act(sg[b][:], psG[b][:], AF.Silu, [f'psG{b}'], [f'sg{b}'])
                    tt('dve', hmT[:, fc, :], sg[b][:], psU[b][:], ALU.mult, [f'sg{b}', f'psU{b}'], ['hmT'])

            def p4_down(gi, t):
                xs = nx[0] % 3
                nx[0] += 1
                d2 = t % 2
                tok0 = gi * 512 + t * 128
                dma('sp', xin[xs][:], x1_scr[tok0:tok0 + 128, :], [], [f'xin{xs}'], f'd_xin{xs}')
                for n in range(2):
                    for fc in range(NFC):
                        mm(psD[d2][:, n * 512:(n + 1) * 512], hmT[:, fc, t * 128:(t + 1) * 128],
                           Wd[:, fc, n * 512:(n + 1) * 512], fc == 0, fc == NFC - 1, ['hmT', f'Wd#{fc // 6}'],
                           [f'psD{d2}'], inc=(fc == NFC - 1 and n == 1))
                return xs

            def p4_post(gi, t, xs):
                d2 = t % 2
                c = 16 + 8 * d2
                tok0 = gi * 512 + t * 128
                SS = f'ssB{d2}'
                act(junk[:], psD[d2][:], AF.Square, [f'psD{d2}'], ['junk', SS], accum=ss[:, c:c + 1])
                rstd_from_ss(ss[:, c:c + 1], ss[:, c + 1:c + 2], ss[:, c + 2:c + 3], 1, EPS, SS)
                stt('dve', xo[d2][:], psD[d2][:], ss[:, c + 2:c + 3], gb_fpost[:], ALU.mult, ALU.mult,
                    [f'psD{d2}', SS, 'gb_fpost'], [f'xo{d2}'])
                tt('pool', xo[d2][:], xo[d2][:], xin[xs][:], ALU.add, [f'xo{d2}', f'xin{xs}'], [f'xo{d2}'])
                dma('sp', x_dst[tok0:tok0 + 128, :], xo[d2][:], [f'xo{d2}'], [], f'd_xo{d2}')

            for t in range(4):
                p4_pre(0, t)
                p4_tr(0, t)
            for ci in range(2, len(FCH)):
                load_gu(ci)
            for ci in range(len(FCD)):
                load_d(ci)
            for gi in range(NG):
                p4_fc(gi)
                for t in range(4):
                    if gi + 1 < NG:
                        p4_pre(gi + 1, t)
                    xs = p4_down(gi, t)
                    if gi + 1 < NG:
                        p4_tr(gi + 1, t)
                    p4_post(gi, t, xs)
            P.barrier()

    P.emit()
    return nc


_CACHE = {}


def _consts():
    return {
        "c_ident": np.eye(128, dtype=np.float32),
        "c_tril": np.tril(np.ones((128, 128), dtype=np.float32)),
        "c_triu": np.triu(np.ones((128, 128), dtype=np.float32)),
    }


def kernel(x, mix_pre_g, w_in, b_forget, sgu_norm_g, w_spatial, b_spatial, w_out,
           mix_post_g, ffn_pre_g, w_gate, w_up, w_down, ffn_post_g):
    B, S, _ = x.shape
    depth = w_in.shape[0]
    key = (S, depth)
    if key not in _CACHE:
        _CACHE[key] = build_program(S, depth)
    nc = _CACHE[key]
    params = {
        "mix_pre_g": mix_pre_g, "w_in": w_in, "b_forget": b_forget, "sgu_norm_g": sgu_norm_g,
        "w_spatial": w_spatial, "b_spatial": b_spatial, "w_out": w_out, "mix_post_g": mix_post_g,
        "ffn_pre_g": ffn_pre_g, "w_gate": w_gate, "w_up": w_up, "w_down": w_down, "ffn_post_g": ffn_post_g,
    }
    params = {k: np.ascontiguousarray(np.asarray(v, dtype=np.float32)) for k, v in params.items()}
    params.update(_consts())
    xs = np.asarray(x, dtype=np.float32)
    in_maps = []
    for b in range(B):
        m = {"x": np.ascontiguousarray(xs[b])}
        m.update(params)
        in_maps.append(m)
    res = run_bass_kernel_spmd(nc, in_maps, core_ids=list(range(B)))
    return np.stack([np.asarray(r["y"], dtype=np.float32) for r in res.results], axis=0)
```
